# Optimizing a Trainium2 kernel written in Bass

```python
import jax, jax.numpy as jnp
from jax import lax
import numpy as np

D_MODEL = 1024
BATCH = 4
SEQ = 8192
DEPTH = 1

ATT_HEADS = 8
ATT_KV_HEADS = 2
ATT_HEAD_DIM = 64
WINDOW = 128
ATT_BLOCK = 128
RET_HEADS = 4
RET_QK_DIM = 64
RET_V_DIM = 128
RET_CHUNK = 128
ROT_BASE = 10000.0

ATT_WIDTH = ATT_HEADS * ATT_HEAD_DIM
ATT_KV_WIDTH = ATT_KV_HEADS * ATT_HEAD_DIM
RET_QK_WIDTH = RET_HEADS * RET_QK_DIM
RET_WIDTH = RET_HEADS * RET_V_DIM
MIX_WIDTH = ATT_WIDTH + RET_WIDTH
IN_SPLITS = (ATT_WIDTH, ATT_KV_WIDTH, ATT_KV_WIDTH, ATT_WIDTH,
             RET_QK_WIDTH, RET_QK_WIDTH, RET_WIDTH, RET_WIDTH)
IN_WIDTH = sum(IN_SPLITS)
RMS_EPS = 1e-6
GN_EPS = 1e-6
NEG_INF = -1e30

kernel_name = "hymba_swa_sink_retention_hybrid"


def rmsnorm(x, g):
    xf = x.astype(jnp.float32)
    y = xf * lax.rsqrt(jnp.mean(xf * xf, axis=-1, keepdims=True) + RMS_EPS)
    return (y * g.astype(jnp.float32)).astype(x.dtype)


def sliding_window_attention(q, k, v, sinks):
    B, S = q.shape[0], q.shape[1]
    N = S // ATT_BLOCK
    G = ATT_HEADS // ATT_KV_HEADS
    q = q.reshape(B, N, ATT_BLOCK, ATT_KV_HEADS, G, ATT_HEAD_DIM)
    k = k.reshape(B, N, ATT_BLOCK, ATT_KV_HEADS, ATT_HEAD_DIM)
    v = v.reshape(B, N, ATT_BLOCK, ATT_KV_HEADS, ATT_HEAD_DIM)

    def with_prev(t):
        prev = jnp.concatenate([jnp.zeros_like(t[:, :1]), t[:, :-1]], axis=1)
        return jnp.concatenate([prev, t], axis=2)

    kk, vv = with_prev(k), with_prev(v)
    scale = ATT_HEAD_DIM ** -0.5
    s = jnp.einsum('bnqhgd,bnkhd->bnhgqk', q, kk).astype(jnp.float32) * scale
    qi = jnp.arange(ATT_BLOCK)[:, None]
    kj = jnp.arange(2 * ATT_BLOCK)[None, :]
    diff = qi + ATT_BLOCK - kj
    band = (diff >= 0) & (diff < WINDOW)
    in_cur = kj >= ATT_BLOCK
    blk = jnp.arange(N)[:, None, None]
    valid = band[None] & ((blk > 0) | in_cur[None])
    s = jnp.where(valid[None, :, None, None], s, NEG_INF)
    sink = jnp.broadcast_to(
        sinks.astype(jnp.float32).reshape(ATT_KV_HEADS, G)[None, None, :, :, None, None],
        s.shape[:-1] + (1,))
    p = jax.nn.softmax(jnp.concatenate([s, sink], axis=-1), axis=-1)[..., :-1]
    o = jnp.einsum('bnhgqk,bnkhd->bnqhgd', p.astype(v.dtype), vv)
    return o.reshape(B, S, ATT_WIDTH)


def rotate_pairs(t, cos, sin):
    B, S, H, D = t.shape
    tf = t.astype(jnp.float32).reshape(B, S, H, D // 2, 2)
    a, b = tf[..., 0], tf[..., 1]
    c, s = cos[None, :, None, :], sin[None, :, None, :]
    out = jnp.stack([a * c - b * s, a * s + b * c], axis=-1)
    return out.reshape(B, S, H, D).astype(t.dtype)


def retention(q, k, v, gn_gain):
    B, S = q.shape[0], q.shape[1]
    C = RET_CHUNK
    N = S // C
    pos = jnp.arange(S, dtype=jnp.float32)
    theta = 1.0 / (ROT_BASE ** jnp.linspace(0.0, 1.0, RET_QK_DIM // 2, dtype=jnp.float32))
    ang = pos[:, None] * theta[None, :]
    cos, sin = jnp.cos(ang), jnp.sin(ang)
    q = rotate_pairs(q, cos, sin)
    k = rotate_pairs(k, cos, sin) * (RET_QK_DIM ** -0.5)

    log_gamma = jnp.log(1.0 - 2.0 ** (-5.0 - jnp.arange(RET_HEADS, dtype=jnp.float32)))
    idx = jnp.arange(C, dtype=jnp.float32)
    rel = idx[:, None] - idx[None, :]
    decay_in = jnp.where(rel >= 0, jnp.exp(log_gamma[:, None, None] * jnp.maximum(rel, 0.0)), 0.0)
    k_dec = jnp.exp(log_gamma[:, None] * (C - 1 - idx)[None, :])
    q_dec = jnp.exp(log_gamma[:, None] * (idx + 1)[None, :])
    chunk_decay = jnp.exp(log_gamma * C)

    q = q.reshape(B, N, C, RET_HEADS, RET_QK_DIM)
    k = k.reshape(B, N, C, RET_HEADS, RET_QK_DIM)
    v = v.reshape(B, N, C, RET_HEADS, RET_V_DIM)

    sc = jnp.einsum('bnchd,bnmhd->bnhcm', q, k) * decay_in[None, None]
    o_inner = jnp.einsum('bnhcm,bnmhe->bnche', sc, v)
    kv_chunk = jnp.einsum('bnmhd,bnmhe,hm->bnhde', k, v, k_dec).astype(jnp.float32)

    def step(state, kv):
        return state * chunk_decay[None, :, None, None] + kv, state

    init = jnp.zeros((B, RET_HEADS, RET_QK_DIM, RET_V_DIM), jnp.float32)
    _, prev = lax.scan(step, init, jnp.moveaxis(kv_chunk, 1, 0))
    prev = jnp.moveaxis(prev, 0, 1)
    o_cross = jnp.einsum('bnchd,bnhde,hc->bnche', q, prev, q_dec)
    o = (o_inner + o_cross).astype(jnp.float32).reshape(B, S, RET_HEADS, RET_V_DIM)

    mu = jnp.mean(o, axis=-1, keepdims=True)
    var = jnp.mean(jnp.square(o - mu), axis=-1, keepdims=True)
    o = (o - mu) * lax.rsqrt(var + GN_EPS)
    o = o.reshape(B, S, RET_WIDTH) * gn_gain.astype(jnp.float32)
    return o.astype(v.dtype)


def setup_inputs(seed: int = 0) -> dict:
    key = jax.random.key(seed)
    ks = jax.random.split(key, 8)
    x = jax.random.normal(ks[0], (BATCH, SEQ, D_MODEL), jnp.float32)
    norm_g = 1.0 + 0.02 * jax.random.normal(ks[1], (DEPTH, D_MODEL), jnp.float32)
    w_in = jax.random.normal(ks[2], (DEPTH, D_MODEL, IN_WIDTH), jnp.float32) * D_MODEL ** -0.5
    att_sinks = 0.5 * jax.random.normal(ks[3], (DEPTH, ATT_HEADS), jnp.float32)
    ret_gn_g = 1.0 + 0.02 * jax.random.normal(ks[4], (DEPTH, RET_WIDTH), jnp.float32)
    w_out = jax.random.normal(ks[5], (DEPTH, MIX_WIDTH, D_MODEL), jnp.float32) * MIX_WIDTH ** -0.5
    final_g = 1.0 + 0.02 * jax.random.normal(ks[6], (D_MODEL,), jnp.float32)
    return {"x": x, "norm_g": norm_g, "w_in": w_in, "att_sinks": att_sinks,
            "ret_gn_g": ret_gn_g, "w_out": w_out, "final_g": final_g}


def reference(x, norm_g, w_in, att_sinks, ret_gn_g, w_out, final_g):
    B, S = x.shape[0], x.shape[1]
    cuts = np.cumsum(IN_SPLITS)[:-1].tolist()
    for l in range(DEPTH):
        h = rmsnorm(x, norm_g[l])
        proj = jnp.einsum('bsd,de->bse', h, w_in[l])
        aq, ak, av, az, rq, rk, rv, rz = jnp.split(proj, cuts, axis=-1)
        a = sliding_window_attention(
            aq.reshape(B, S, ATT_HEADS, ATT_HEAD_DIM),
            ak.reshape(B, S, ATT_KV_HEADS, ATT_HEAD_DIM),
            av.reshape(B, S, ATT_KV_HEADS, ATT_HEAD_DIM),
            att_sinks[l]) * jax.nn.silu(az)
        r = retention(
            rq.reshape(B, S, RET_HEADS, RET_QK_DIM),
            rk.reshape(B, S, RET_HEADS, RET_QK_DIM),
            rv.reshape(B, S, RET_HEADS, RET_V_DIM),
            ret_gn_g[l]) * jax.nn.silu(rz)
        mix = jnp.concatenate([a, r], axis=-1)
        x = x + jnp.einsum('bse,ed->bsd', mix, w_out[l])
    return rmsnorm(x, final_g)
```

```python
import math
from contextlib import ExitStack

import numpy as np
import ml_dtypes

import concourse.bass as bass
import concourse.mybir as mybir
from concourse.bass_utils import run_bass_kernel_spmd

F32 = mybir.dt.float32
BF16 = mybir.dt.bfloat16
AF = mybir.ActivationFunctionType
ALU = mybir.AluOpType
AX = mybir.AxisListType

D = 1024
NCORES = 8
SEQ = 8192
HALF = SEQ // 2
CH = 128
RMS_EPS = 1e-6
GN_EPS = 1e-6
NEG = -30000.0
ATT_SCALE = 64 ** -0.5
SCHEDULE = True
DEFAULT_OPTS = dict(
    pools={"A": [0, 1, 2], "B": [3, 4, 5, 6]},
    order=[("load", -4), ("M", 2), ("scT", 1), ("PV", 1), ("OR", 1), ("KV", 1), ("O", 2), ("R", 0), ("SC", 0),
           ("T", -2), ("P", -1), ("P2", -1), ("front", -3)],
    pre_lead=1,
    window=40,
    urgency=True,
    bin_us=0.5,
    bias={"dve": 1.0},
    bias_pre={"dve": 0.4},
    hscale_main=("act", "dve", "pool"),
    hscale_pre=("dve",),
    use_pool=False,
    mask_pool=False,
)

A_AK, A_AV, A_RQ, A_RK, A_RV = 0, 128, 256, 512, 768
B_AQ, B_AZ, B_RZ = 0, 512, 1024
WA = 1280
WB = 1536


class Buf:
    __slots__ = ("name", "last_w", "readers", "excl", "owner")

    def __init__(self, name, excl=False):
        self.owner = None
        self.name = name
        self.last_w = None
        self.readers = []
        self.excl = excl


class Op:
    __slots__ = ("eng", "fn", "deps", "signals", "val", "queue", "inc", "wait_all", "idx", "cost", "t0", "t1",
                 "urg", "alts", "is_dma", "tag")

    def __init__(self, alts, queue, inc, wait_all, is_dma):
        self.alts = alts
        self.eng = next(iter(alts)) if len(alts) == 1 else None
        self.fn = alts[self.eng] if self.eng else None
        self.deps = []
        self.signals = False
        self.val = None
        self.queue = queue
        self.inc = inc
        self.wait_all = wait_all
        self.is_dma = is_dma
        self.idx = 0
        self.cost = {}
        self.t0 = None
        self.t1 = None
        self.urg = 0


class Sched:
    ENGS = ("pe", "act", "dve", "pool", "sp")
    CROSS_CHUNK = ("kTa", "Vaug", "Sbfp", "S", "Mk", "const", "misc", "W", "wstage", "c", "junk")

    def __init__(self):
        self.cur_chunk = None
        self.streams = {e: [] for e in self.ENGS}
        self.queues = {}
        self.n_ops = 0
        self.cur_tag = 'init'

    def add(self, eng, fn=None, reads=(), writes=(), dmaq=None, wait_all=False, cost=None):
        if isinstance(eng, str):
            alts = {eng: fn}
        elif isinstance(eng, dict):
            alts = dict(eng)
        else:
            alts = {e: fn for e in eng}
        if dmaq is None:
            op = Op(alts, None, 1, False, False)
        else:
            op = Op(alts, dmaq, 16, wait_all, True)
        deps = []
        seen = set()
        is_pe = op.eng == "pe"

        def dep(o):
            if o is None or o is op or id(o) in seen:
                return
            if is_pe and o.eng == "pe" and not o.is_dma and dmaq is None:
                return
            seen.add(id(o))
            deps.append(o)

        for b in reads:
            dep(b.last_w)
            if b.excl:
                for r in b.readers:
                    if r.eng is None or op.eng is None or r.eng != op.eng:
                        dep(r)
        for b in writes:
            dep(b.last_w)
            for r in b.readers:
                dep(r)
        for b in reads:
            b.readers.append(op)
            if (b.owner is not None and self.cur_chunk is not None and b.owner != self.cur_chunk
                    and not b.name.startswith(self.CROSS_CHUNK)):
                raise AssertionError(f"slot {b.name} holds chunk {b.owner} but stage {self.cur_tag} of chunk "
                                     f"{self.cur_chunk} reads it")
        for b in writes:
            b.last_w = op
            b.readers = []
            b.owner = self.cur_chunk
        pe_deps = [o for o in deps if o.eng == "pe" and not o.is_dma]
        if len(pe_deps) > 1:
            youngest = max(pe_deps, key=lambda o: o.idx)
            deps = [o for o in deps if not (o.eng == "pe" and not o.is_dma) or o is youngest]
        for o in deps:
            o.signals = True
        op.deps = deps
        if dmaq is not None:
            op.signals = True
        op.idx = self.n_ops
        op.tag = self.cur_tag
        if cost is not None:
            op.cost = dict(cost)
        self.n_ops += 1
        for e in alts:
            self.streams[e].append(op)
        return op

    def estimate_costs(self):
        class Rec:
            def __init__(self):
                self.name = None
                self.args = None
                self.kw = None

            def __getattr__(self, name):
                def f(*a, **kw):
                    self.name, self.args, self.kw = name, a, kw
                    return None
                return f

        def free_size(ap):
            n = 1
            for d in ap.shape[1:]:
                n *= d
            return n

        seen = set()
        for eng, stream in self.streams.items():
            for op in stream:
                if id(op) in seen:
                    continue
                seen.add(id(op))
                for e, fn in op.alts.items():
                    if e in op.cost:
                        continue
                    r = Rec()
                    fn(r)
                    kw, a = r.kw, r.args
                    if e == "pe":
                        n = free_size(kw["rhs"]) if r.name == "matmul" else free_size(a[1])
                        c = max(0.058, n / 2160.0) + (0.024 if n == 256 else 0.0)
                    elif e == "sp":
                        c = 0.1
                    else:
                        out = kw.get("out", a[0] if a else None)
                        n = free_size(out) if out is not None else 128
                        if e == "act":
                            c = 0.22 + n * 0.0009
                        elif e == "dve":
                            c = 0.13 + n * 0.0011
                        else:
                            c = 0.45 + n * 0.0017
                    op.cost[e] = c

    def schedule(self, window=40, lat=0.2, dma_lat=4.5, urgency=True, bin_us=0.5, bias=None, bias_pre=None):
        self.bin_us = bin_us
        self.bias = bias or {}
        self.bias_pre = bias_pre or {}
        self.estimate_costs()
        allops = {}
        for st_ in self.streams.values():
            for o in st_:
                allops[id(o)] = o
        allops = sorted(allops.values(), key=lambda o: o.idx)
        INF = 1 << 60
        for o in allops:
            o.urg = o.idx if o.eng == "pe" else INF
        for o in reversed(allops):
            for d in o.deps:
                if o.urg < d.urg:
                    d.urg = o.urg
        if not urgency:
            for o in allops:
                o.urg = o.idx
        ptr = {e: 0 for e in self.ENGS}
        free_at = {e: 0.0 for e in self.ENGS}
        remaining = len(allops)

        def ready_time(op):
            t = 0.0
            for d in op.deps:
                if d.t1 is None:
                    return None
                t = max(t, d.t1 + lat)
            return t

        BIN = self.bin_us
        while remaining:
            best = None
            cand = {}
            for e in self.ENGS:
                stream = self.streams[e]
                n = len(stream)
                p = ptr[e]
                while p < n and stream[p].t0 is not None:
                    p += 1
                ptr[e] = p
                if p >= n:
                    continue
                w = 1 if e in ("pe", "sp") else window
                cnt = 0
                i = p
                while i < n and cnt < w:
                    op = stream[i]
                    if op.t0 is None:
                        cnt += 1
                        rt = ready_time(op)
                        if rt is not None:
                            st = max(rt, free_at[e])
                            bias = self.bias_pre if op.tag.endswith("_pre") else self.bias
                            fin = st + op.cost[e] + (bias.get(e, 0.0) if len(op.alts) > 1 else 0.0)
                            c = cand.get(id(op))
                            if c is None or fin < c[0]:
                                cand[id(op)] = (fin, st, e, op)
                    i += 1
            for fin, st, e, op in cand.values():
                key = (int(st / BIN), op.urg, st, op.idx)
                if best is None or key < best[0]:
                    best = (key, e, op, st)
            assert best is not None, "scheduler deadlock"
            _, e, op, st = best
            op.t0 = st
            op.eng = e
            op.fn = op.alts[e]
            if e == "sp":
                free_at[e] = st + op.cost[e]
                op.t1 = st + dma_lat
            else:
                op.t1 = st + op.cost[e]
                free_at[e] = op.t1
            remaining -= 1
        for e in self.ENGS:
            self.streams[e] = sorted((o for o in self.streams[e] if o.eng == e), key=lambda o: (o.t0, o.idx))
        self.makespan = max(o.t1 for o in allops)

    def finalize(self):
        self.totals = {}
        self.queues = {}
        for e in self.ENGS:
            for op in self.streams[e]:
                if op.eng is None:
                    op.eng = e
                    op.fn = op.alts[e]
                if op.eng != e:
                    continue
                if not op.is_dma:
                    op.queue = e
                self.queues.setdefault(op.queue, []).append(op)
            self.streams[e] = [o for o in self.streams[e] if o.eng == e]
        for q, ops in self.queues.items():
            n = 0
            for op in ops:
                if op.signals:
                    n += op.inc
                    op.val = n
            self.totals[q] = n
            for op in ops:
                if op.wait_all:
                    op.val = n

    def emit(self, nc, block, sems, final_waits):
        engmap = {"pe": block.tensor, "act": block.scalar, "dve": block.vector,
                  "pool": block.gpsimd, "sp": block.sync}
        for eng in self.ENGS:
            stream = self.streams[eng]

            def body(e, stream=stream, eng=eng):
                waited = {}
                for op in stream:
                    for d in op.deps:
                        if waited.get(d.queue, 0) < d.val:
                            e.wait_ge(sems[d.queue], d.val)
                            waited[d.queue] = d.val
                    ins = op.fn(e)
                    if op.signals:
                        ins.then_inc(sems[op.queue], op.inc)
                if eng == "sp":
                    for q in final_waits:
                        if self.totals.get(q, 0) > 0:
                            e.wait_ge(sems[q], self.totals[q])

            engmap[eng](body)


def build_program(NP, NM, opts=None):
    NT = NP + NM
    opts = dict(DEFAULT_OPTS, **(opts or {}))
    nc = bass.Bass("TRN2", target_bir_lowering=False)
    S = Sched()
    es = ExitStack()

    def dram(name, shape, dt, kind="ExternalInput"):
        return nc.dram_tensor(name, list(shape), dt, kind=kind).ap()

    x_d = dram("x_all", [NT * CH, D], F32)
    tab_d = dram("tabs", [NT * CH, 512], F32)
    wa_d = dram("w_a", [D, WA], F32)
    wb_d = dram("w_b", [D, WB], F32)
    wo_d = dram("w_o", [D, D], F32)
    g_d = dram("g_col", [128, 8], F32)
    fg_d = dram("fg_rep", [128, D], F32)
    gng_d = dram("gng_rep", [128, 512], F32)
    snk_d = dram("sinks_rep", [128, 8], F32)
    msk_d = dram("masks", [128, 3 * 512], BF16)
    cz_d = dram("causal01", [128, 512], F32)
    cd_d = dram("cdtab", [128, 256], F32)
    idb_d = dram("ident_bf", [128, 128], BF16)
    gs_d = dram("gsel", [128, 256], BF16)
    out_d = dram("out", [NM * CH, D], F32, kind="ExternalOutput")

    def sb(name, shape, dt):
        return es.enter_context(nc.sbuf_tensor(name, list(shape), dt))

    Wa = sb("Wa", [128, 8, WA], BF16)
    Wb = sb("Wb", [128, 8, WB], BF16)
    Wo = sb("Wo", [128, 8, D], BF16)
    g_col = sb("g_col_s", [128, 8], F32)
    gngh = sb("gngh", [128, 512], F32)
    fg_rep = sb("fg_rep_s", [128, D], F32)
    gng_rep = sb("gng_rep_s", [128, 512], F32)
    sinks = sb("sinks_s", [128, 8], F32)
    negsinkmax = sb("negsinkmax", [128, 1], F32)
    masks = sb("masks_s", [128, 3 * 512], BF16)
    causal = sb("causal_s", [128, 512], F32)
    cdtab = sb("cdtab_s", [128, 256], F32)
    ident_bf = sb("ident_bf_s", [128, 128], BF16)
    gsel = sb("gsel_s", [128, 256], BF16)
    chalf = sb("chalf", [128, 8], F32)
    neghalf = sb("neghalf", [128, 8], F32)
    cst = sb("cst", [128, 8], F32)
    sinkmax = sb("sinkmax", [128, 1], F32)
    ones_bf = sb("ones_bf", [128, 128], BF16)
    sinks2 = sb("sinks2", [128, 8], F32)
    b_const = Buf("const")
    b_Wa_kv, b_Wa_rest, b_Wb, b_Wo = ([Buf(f"{n}{k}") for k in range(8)] for n in ("Wakv", "War", "Wb", "Wo"))
    b_Wa = b_Wa_kv + b_Wa_rest

    NST = 3
    wstage = [sb(f"wstage{i}", [128, max(WA, WB)], F32) for i in range(NST)]
    b_wstage = [Buf(f"wstage{i}") for i in range(NST)]

    def slots(name, n, shape, dt):
        return [sb(f"{name}{i}", shape, dt) for i in range(n)], [Buf(f"{name}{i}") for i in range(n)]

    NX = 7
    xs, b_xs = slots("xs", NX, [128, D], F32)
    NTB = 5
    tb, b_tb = slots("tb", NTB, [128, 512], F32)
    junk = sb("junk", [128, D], BF16)
    b_junk = Buf("junk")
    stat, b_stat = slots("stat", 4, [128, 4], F32)
    hh, b_hh = slots("hh", 2, [128, D], BF16)
    hT, b_hT = slots("hT", 2, [128, D], BF16)
    aqT, b_aqT = slots("aqT", 2, [128, 512], BF16)
    NKV = 4
    kTa = [[sb(f"kTa{i}_{g}", [128, 128], BF16) for g in range(2)] for i in range(NKV)]
    b_kTa = [Buf(f"kTa{i}") for i in range(NKV)]
    Vaug, b_Vaug = slots("Vaug", NKV, [128, 2, 65], BF16)
    rotu, b_rotu = slots("rotu", 1, [128, 512], F32)
    rotw, b_rotw = slots("rotw", 1, [128, 512], F32)
    qk, b_qk = slots("qk", 3, [128, 512], BF16)
    qT, b_qT = slots("qT", 2, [128, 2, 128], BF16)
    kTp = [[sb(f"kTp{i}_{r}", [128, 2, 128], BF16) for r in range(2)] for i in range(2)]
    b_kTp = [Buf(f"kTp{i}") for i in range(2)]
    rv, b_rv = slots("rv", 3, [128, 512], BF16)
    gate, b_gate = slots("gate", 2, [128, 1024], F32)
    PT = [[sb(f"PT{i}_{u}", [128, 512], BF16) for u in range(4)] for i in range(2)]
    b_PT = [[Buf(f"PT{i}_{u}") for u in range(4)] for i in range(2)]
    sqq, b_sqq = slots("sqq", 1, [128, 512], BF16)
    sqk, b_sqk = slots("sqk", 1, [128, 128], BF16)
    Mq, b_Mq = slots("Mq", 2, [128, 2], F32)
    Mk, b_Mk = slots("Mk", 3, [128, 2], F32)
    tmpc, b_tmpc = slots("tmpc", 2, [128, 8], F32)
    negC, b_negC = slots("negC", 2, [128, 2], F32)
    orsb, b_orsb = slots("orsb", 1, [128, 512], F32)
    esk, b_esk = slots("esk", 2, [128, 8], F32)
    scD, b_scD = slots("scD", 2, [128, 512], BF16)
    Sst = sb("Sst", [128, 2, 128], F32)
    b_S = Buf("S")
    Sbfp = [[sb(f"Sbfp{i}_{r}", [128, 2, 128], BF16) for r in range(2)] for i in range(2)]
    b_Sbfp = [Buf(f"Sbfp{i}") for i in range(2)]
    den, b_den = slots("den", 2, [128, 16], F32)
    tmpa, b_tmpa = slots("tmpa", 1, [128, 512], F32)
    gns, b_gns = slots("gns", 2, [128, 48], F32)
    gg, b_gg = slots("gg", 1, [128, 512], F32)
    mix, b_mix = slots("mix", 2, [128, D], BF16)
    mixT, b_mixT = slots("mixT", 2, [128, D], BF16)
    yy, b_yy = slots("yy", 2, [128, D], F32)
    stat2, b_stat2 = slots("stat2", 2, [128, 4], F32)

    banks = [es.enter_context(nc.psum_tensor(f"ps{i}", [128, 512], F32)) for i in range(8)]
    b_bank = [Buf(f"bank{i}", excl=True) for i in range(8)]
    b_smallA, b_smallB = Buf("smallA", excl=True), Buf("smallB", excl=True)
    pools = {k: list(v) for k, v in opts["pools"].items()}
    rr = {"A": 0, "B": 0}

    def alloc(pool):
        lst = pools[pool]
        i = lst[rr[pool] % len(lst)]
        rr[pool] += 1
        return banks[i], b_bank[i]

    qnames = ["pe", "act", "dve", "pool", "const", "constB"]
    qnames += [f"ws{i}" for i in range(NST)]
    qnames += [f"xl{i}" for i in range(NX)] + [f"tl{i}" for i in range(5)] + [f"st{i}" for i in range(2)]
    sems = {q: es.enter_context(nc.semaphore(q)) for q in qnames}
    block = es.enter_context(nc.Block())

    def dma(eng_ap_out, eng_ap_in):
        return lambda e: e.dma_start(out=eng_ap_out, in_=eng_ap_in)

    def _flt(engines):
        if not opts["use_pool"]:
            e2 = tuple(x for x in engines if x != "pool")
            return e2 if e2 else engines
        return engines

    def cp(out, in_, engines=("act", "dve")):
        engines = _flt(engines)
        d = {}
        for en in engines:
            if en == "act":
                d[en] = lambda e: e.activation(out=out, in_=in_, func=AF.Copy)
            else:
                d[en] = lambda e: e.tensor_copy(out=out, in_=in_)
        return d

    def tt(out, in0, in1, op, engines=("dve", "pool")):
        engines = _flt(engines)
        return {en: (lambda e: e.tensor_tensor(out=out, in0=in0, in1=in1, op=op)) for en in engines}

    def scale_rows(out, in_, col, n, engines=("act", "dve", "pool")):
        engines = _flt(engines)
        d = {}
        for en in engines:
            if en == "act":
                d[en] = lambda e: e.activation(out=out, in_=in_, func=AF.Copy, scale=col)
            elif en == "dve":
                d[en] = lambda e: e.tensor_scalar(out=out, in0=in_, scalar1=col, scalar2=None, op0=ALU.mult)
            else:
                d[en] = lambda e: e.tensor_tensor(out=out, in0=in_, in1=col.to_broadcast([128, n]), op=ALU.mult)
        return d

    consts = [(ident_bf, idb_d), (g_col, g_d), (sinks, snk_d), (cdtab, cd_d)]
    consts_late = [(masks, msk_d), (causal, cz_d), (gng_rep, gng_d), (fg_rep, fg_d), (gsel, gs_d)]
    for t, d_ in consts:
        last_const = S.add("sp", dma(t[:], d_[:, :]), writes=[Buf("c")], dmaq="const", wait_all=True)
    b_const.last_w = last_const
    last_const.signals = True
    b_constB = Buf("constB")

    def emit_late_consts():
        for t, d_ in consts_late:
            lc = S.add("sp", dma(t[:], d_[:, :]), writes=[Buf("c")], dmaq="constB", wait_all=True)
        b_constB.last_w = lc
        lc.signals = True
    b_misc = Buf("misc")
    b_gngh = Buf("misc_gngh")
    S.add("pool", lambda e: e.memset(neghalf[:], -0.5), writes=[b_misc])
    S.add("pool", lambda e: e.memset(chalf[:], 0.5), writes=[b_misc])
    S.add("pool", lambda e: e.memset(cst[:, 0:1], -ATT_SCALE), writes=[b_misc])
    S.add("pool", lambda e: e.memset(cst[:, 1:2], -1.0), writes=[b_misc])
    S.add("pool", lambda e: e.memset(ones_bf[:], 1.0), writes=[b_misc])
    S.add("pool", lambda e: e.memset(cst[:, 2:3], -0.5 * ATT_SCALE), writes=[b_misc])

    def memset_all(t, val, buf, eng="pool"):
        S.add(eng, lambda e, t=t, val=val: e.memset(t[:], val), writes=[buf])

    for i in range(NKV):
        for g in range(2):
            memset_all(kTa[i][g], 0.0, b_kTa[i])
        memset_all(Vaug[i], 2.0, b_Vaug[i])
    for i in range(2):
        for r in range(2):
            memset_all(kTp[i][r], 0.0, b_kTp[i])
            memset_all(Sbfp[i][r], 0.0, b_Sbfp[i])
    memset_all(Sst, 0.0, b_S)
    S.add("dve", lambda e: e.tensor_scalar(out=sinks2[:], in0=sinks[:], scalar1=math.log(2.0), scalar2=None, op0=ALU.add),
          reads=[b_const], writes=[b_misc])
    S.add("dve", lambda e: e.tensor_reduce(out=sinkmax[:], in_=sinks[:], axis=AX.X, op=ALU.max),
          reads=[b_const], writes=[b_misc])
    S.add("dve", lambda e: e.tensor_scalar(out=negsinkmax[:], in0=sinkmax[:], scalar1=-1.0, scalar2=None,
                                           op0=ALU.mult),
          reads=[b_misc], writes=[b_misc])

    def emit_gngh():
        S.add("dve", lambda e: e.tensor_scalar(out=gngh[:], in0=gng_rep[:], scalar1=0.5, scalar2=None, op0=ALU.mult),
              reads=[b_constB], writes=[b_gngh])

    cast_rr = [0]

    def load_weight(dst, b_dst, src_d, c0, c1, kb, scaled):
        width = c1 - c0
        si = cast_rr[0] % NST
        cast_rr[0] += 1
        st_t, b_st = wstage[si], b_wstage[si]
        S.add("sp", lambda e: e.dma_start(out=st_t[:, 0:width], in_=src_d[kb * 128:(kb + 1) * 128, c0:c1]),
              writes=[b_st], dmaq=f"ws{si}")
        if scaled:
            alts = scale_rows(dst[:, kb, c0:c1], st_t[:, 0:width], g_col[:, kb:kb + 1], width)
        else:
            alts = cp(dst[:, kb, c0:c1], st_t[:, 0:width], ("act", "dve", "pool"))
        S.add(alts, reads=[b_st, b_const], writes=[b_dst[kb]])

    def weight_tasks(dst, b_dst, src_d, c0, c1, scaled):
        return [(dst, b_dst, src_d, c0, c1, kb, scaled) for kb in range(8)]

    early_w = weight_tasks(Wa, b_Wa_kv, wa_d, A_RK, WA, True)
    pending_w = (weight_tasks(Wa, b_Wa_rest, wa_d, 0, A_RK, True) + weight_tasks(Wb, b_Wb, wb_d, 0, WB, True)
                 + weight_tasks(Wo, b_Wo, wo_d, 0, D, False))


    def is_main(c):
        return c >= NP

    def st_load(c):
        sx = c % NX
        S.add("sp", dma(xs[sx][:], x_d[c * CH:(c + 1) * CH, :]), writes=[b_xs[sx]], dmaq=f"xl{sx}")
        S.add("sp", dma(tb[c % NTB][:], tab_d[c * CH:(c + 1) * CH, :]), writes=[b_tb[c % NTB]], dmaq=f"tl{c % NTB}")

    def st_front(c):
        sx, st = c % NX, c % 4
        S.add("act", lambda e: e.activation(out=hh[c % 2][:], in_=xs[sx][:], func=AF.Square,
                                            accum_out=stat[st][:, 0:1]),
              reads=[b_xs[sx]], writes=[b_hh[c % 2], b_stat[st]])
        S.add("dve", lambda e: e.tensor_scalar(out=stat[st][:, 1:2], in0=stat[st][:, 0:1], scalar1=1.0 / D,
                                               scalar2=RMS_EPS, op0=ALU.mult, op1=ALU.add),
              reads=[b_stat[st]], writes=[b_stat[st]])
        S.add("pool", lambda e: e.tensor_tensor(out=stat[st][:, 2:3], in0=stat[st][:, 1:2],
                                                in1=neghalf[:, 0:1], op=ALU.pow),
              reads=[b_stat[st], b_misc], writes=[b_stat[st]])
        S.add(scale_rows(hh[c % 2][:], xs[sx][:], stat[st][:, 2:3], D,
                         engines=(opts["hscale_main"] if is_main(c) else opts["hscale_pre"])),
              reads=[b_xs[sx], b_stat[st]], writes=[b_hh[c % 2]])

    def st_T(c):
        bank, bb = alloc("A")
        pb = bank[:].bitcast(BF16)
        for kb in range(8):
            S.add("pe", lambda e, kb=kb: e.transpose(pb[:, kb * 128:(kb + 1) * 128],
                                                     hh[c % 2][:, kb * 128:(kb + 1) * 128], ident_bf[:]),
                  reads=[b_hh[c % 2], b_const], writes=[bb])
        S.add(cp(hT[c % 2][:], pb[:, 0:1024]), reads=[bb], writes=[b_hT[c % 2]])

    def mm_tok(bank, bb, c0, W, b_W, w0, n, c):
        for kb in range(8):
            S.add("pe", lambda e, kb=kb: e.matmul(bank[:, c0:c0 + n], lhsT=hT[c % 2][:, kb * 128:(kb + 1) * 128],
                                                  rhs=W[:, kb, w0:w0 + n], start=(kb == 0), stop=(kb == 7)),
                  reads=[b_hT[c % 2]] + b_W, writes=[bb])

    def mm_feat(bank, bb, c0, W, b_W, w0, c):
        for kb in range(8):
            S.add("pe", lambda e, kb=kb: e.matmul(bank[:, c0:c0 + 128], lhsT=W[:, kb, w0:w0 + 128],
                                                  rhs=hT[c % 2][:, kb * 128:(kb + 1) * 128],
                                                  start=(kb == 0), stop=(kb == 7)),
                  reads=[b_hT[c % 2]] + b_W, writes=[bb])

    def ev_kv(c, bank, bb):
        sk = c % NKV
        for g in range(2):
            S.add(cp(kTa[sk][g][g * 64:(g + 1) * 64, :], bank[g * 64:(g + 1) * 64, 0:128]),
                  reads=[bb], writes=[b_kTa[sk]])
        S.add(cp(Vaug[sk][:, :, 0:64], bank[:, 128:256].rearrange("p (g d) -> p g d", g=2)),
              reads=[bb], writes=[b_Vaug[sk]])

    def ev_rot(c, bank, bb, nqk):
        s2, t3, q3 = c % 2, c % NTB, c % 3
        if nqk == 2:
            src = bank[:, 0:512].rearrange("p (g t i) -> p g t i", g=8, t=2)
            cC = tb[t3][:, 0:256].rearrange("p (g i) -> p g i", g=8)
            cS = tb[t3][:, 256:512].rearrange("p (g i) -> p g i", g=8)
            ng, w0 = 8, 0
        else:
            src = bank[:, 0:256].rearrange("p (g t i) -> p g t i", g=4, t=2)
            cC = tb[t3][:, 128:256].rearrange("p (g i) -> p g i", g=4)
            cS = tb[t3][:, 384:512].rearrange("p (g i) -> p g i", g=4)
            ng, w0 = 4, 256
        wd = ng * 64
        u = rotu[0][:, 0:wd].rearrange("p (g t i) -> p g t i", g=ng, t=2)
        w = rotw[0][:, 0:wd].rearrange("p (g t i) -> p g t i", g=ng, t=2)
        o = qk[q3][:, w0:w0 + wd].rearrange("p (g t i) -> p g t i", g=ng, t=2)
        S.add("dve", lambda e: e.tensor_tensor(out=u, in0=src, in1=cC.unsqueeze(2).to_broadcast([128, ng, 2, 32]),
                                               op=ALU.mult),
              reads=[bb, b_tb[t3]], writes=[b_rotu[0]])
        S.add("dve", lambda e: e.tensor_tensor(out=w, in0=src, in1=cS.unsqueeze(2).to_broadcast([128, ng, 2, 32]),
                                               op=ALU.mult),
              reads=[bb, b_tb[t3]], writes=[b_rotw[0]])
        S.add(tt(o[:, :, 0, :], u[:, :, 0, :], w[:, :, 1, :], ALU.subtract),
              reads=[b_rotu[0], b_rotw[0]], writes=[b_qk[q3]])
        S.add(tt(o[:, :, 1, :], w[:, :, 0, :], u[:, :, 1, :], ALU.add),
              reads=[b_rotu[0], b_rotw[0]], writes=[b_qk[q3]])

    def ev_rv(c, bank, bb, c0=0):
        S.add(cp(rv[c % 3][:], bank[:, c0:c0 + 512]), reads=[bb], writes=[b_rv[c % 3]])

    def norm_q(c, g):
        if g != 0:
            return
        small = banks[7]
        S.add("pe", lambda e: e.matmul(small[:, 0:512], lhsT=ones_bf[:], rhs=sqq[0][:], start=True, stop=True),
              reads=[b_sqq[0], b_misc], writes=[b_smallA])
        S.add("dve", lambda e: e.tensor_reduce(out=Mq[c % 2][:, 0:1], in_=small[:, 0:512], axis=AX.X, op=ALU.max),
              reads=[b_smallA], writes=[b_Mq[c % 2]])

    def norm_k(c):
        small = banks[7]
        S.add("pe", lambda e: e.matmul(small[:, 0:128], lhsT=ones_bf[:], rhs=sqk[0][:], start=True, stop=True),
              reads=[b_sqk[0], b_misc], writes=[b_smallA])
        S.add("dve", lambda e: e.tensor_reduce(out=Mk[c % 3][:, 0:1], in_=small[:, 0:128], axis=AX.X, op=ALU.max),
              reads=[b_smallA], writes=[b_Mk[c % 3]])

    def ev_sqk(c, bank, bb):
        S.add("act", lambda e: e.activation(out=sqk[0][:], in_=bank[:, 0:128], func=AF.Square),
              reads=[bb], writes=[b_sqk[0]])

    def shift_chain(c):
        s2 = c % 2
        T_ = tmpc[s2]
        S.add("dve", lambda e: e.tensor_tensor(out=T_[:, 0:1], in0=Mk[c % 3][:, 0:1], in1=Mk[(c - 1) % 3][:, 0:1],
                                               op=ALU.max),
              reads=[b_Mk[c % 3], b_Mk[(c - 1) % 3]], writes=[b_tmpc[s2]])
        S.add("dve", lambda e: e.tensor_tensor(out=T_[:, 2:3], in0=T_[:, 0:1], in1=Mq[s2][:, 0:1], op=ALU.add),
              reads=[b_tmpc[s2], b_Mq[s2]], writes=[b_tmpc[s2]])
        S.add("dve", lambda e: e.tensor_scalar(out=negC[s2][:, 0:1], in0=T_[:, 2:3], scalar1=cst[:, 2:3],
                                               scalar2=negsinkmax[:], op0=ALU.mult, op1=ALU.min),
              reads=[b_tmpc[s2], b_misc], writes=[b_negC[s2]])
        S.add("act", lambda e: e.activation(out=esk[s2][:], in_=sinks2[:], func=AF.Exp, bias=negC[s2][:, 0:1],
                                            scale=1.0),
              reads=[b_const, b_misc, b_negC[s2]], writes=[b_esk[s2]])

    def st_P_prefix(c):
        bank, bb = alloc("A")
        mm_tok(bank, bb, 0, Wa, b_Wa, A_RK, 256, c)
        ev_rot(c, bank, bb, 1)
        bank, bb = alloc("A")
        mm_tok(bank, bb, 0, Wa, b_Wa, A_RV, 512, c)
        ev_rv(c, bank, bb)
        if c == NP - 1:
            bank, bb = alloc("A")
            mm_feat(bank, bb, 0, Wa, b_Wa, A_AK, c)
            mm_tok(bank, bb, 128, Wa, b_Wa, A_AV, 128, c)
            ev_kv(c, bank, bb)
            ev_sqk(c, bank, bb)
            norm_k(c)

    def st_P_main(c, part):
        s2 = c % 2
        if part == 1:
            bank, bb = alloc("A")
            for j in range(4):
                mm_feat(bank, bb, j * 128, Wb, b_Wb, B_AQ + j * 128, c)
            S.add(cp(aqT[s2][:], bank[:, 0:512]), reads=[bb], writes=[b_aqT[s2]])
            S.add("act", lambda e, bank=bank: e.activation(out=sqq[0][:], in_=bank[:, 0:512], func=AF.Square),
                  reads=[bb], writes=[b_sqq[0]])
            bank, bb = alloc("A")
            mm_feat(bank, bb, 0, Wa, b_Wa, A_AK, c)
            mm_tok(bank, bb, 128, Wa, b_Wa, A_AV, 128, c)
            ev_kv(c, bank, bb)
            ev_sqk(c, bank, bb)
            bank, bb = alloc("A")
            mm_tok(bank, bb, 0, Wa, b_Wa, A_RQ, 512, c)
            ev_rot(c, bank, bb, 2)
            bank, bb = alloc("A")
            mm_tok(bank, bb, 0, Wa, b_Wa, A_RV, 512, c)
            ev_rv(c, bank, bb)
            norm_q(c, 0)
            return
        for gi, w0 in ((0, B_AZ), (1, B_RZ)):
            bank, bb = alloc("A")
            mm_tok(bank, bb, 0, Wb, b_Wb, w0, 512, c)
            S.add("act", lambda e, bank=bank, gi=gi: e.activation(out=gate[s2][:, gi * 512:(gi + 1) * 512],
                                                                  in_=bank[:, 0:512], func=AF.Tanh, scale=0.5),
                  reads=[bb], writes=[b_gate[s2]])
            S.add("dve", lambda e, bank=bank, gi=gi: e.scalar_tensor_tensor(
                out=gate[s2][:, gi * 512:(gi + 1) * 512], in0=gate[s2][:, gi * 512:(gi + 1) * 512], scalar=1.0,
                in1=bank[:, 0:512], op0=ALU.add, op1=ALU.mult),
                reads=[bb, b_gate[s2]], writes=[b_gate[s2]])
            if gi == 0:
                norm_q(c, 1)
            else:
                norm_k(c)
                shift_chain(c)

    def st_R(c):
        s2 = c % 2
        bank, bb = alloc("A")
        pb = bank[:].bitcast(BF16)
        for j in range(4):
            S.add("pe", lambda e, j=j: e.transpose(pb[:, j * 128:(j + 1) * 128], qk[c % 3][:, j * 128:(j + 1) * 128],
                                                   ident_bf[:]),
                  reads=[b_qk[c % 3], b_const], writes=[bb])
        S.add(cp(qT[s2][:].rearrange("p a b -> p (a b)"), pb[:, 0:256]), reads=[bb], writes=[b_qT[s2]])
        for r in range(2):
            S.add(cp(kTp[s2][r][r * 64:(r + 1) * 64, :, :].rearrange("p a b -> p (a b)"),
                     pb[r * 64:(r + 1) * 64, 256:512]),
                  reads=[bb], writes=[b_kTp[s2]])

    def st_SC(c):
        s2 = c % 2
        m = c - NP
        sk_cur, sk_prev = c % NKV, (c - 1) % NKV
        for g in range(2):
            for blk in range(2):
                u = g * 2 + blk
                bank, bb = alloc("B")
                if blk == 0:
                    mk = masks[:, 0:512] if m == 0 else masks[:, 512:1024]
                    sk = sk_prev
                else:
                    mk = masks[:, 1024:1536]
                    sk = sk_cur
                use_mm_mask = (m == 0 and blk == 0) or not opts["mask_pool"]
                if use_mm_mask:
                    S.add("pe", lambda e, bank=bank, mk=mk: e.matmul(bank[:, 0:512], lhsT=ident_bf[:], rhs=mk,
                                                                     start=True, stop=False),
                          reads=[b_const, b_constB], writes=[bb])
                S.add("pe", lambda e, bank=bank, sk=sk, g=g, um=use_mm_mask: e.matmul(
                    bank[:, 0:512], lhsT=kTa[sk][g][:], rhs=aqT[s2][:], start=(not um), stop=True),
                    reads=[b_kTa[sk], b_aqT[s2]], writes=[bb])
                S.add("act", lambda e, u=u, bank=bank, g=g: e.activation(out=PT[s2][u][:], in_=bank[:, 0:512], func=AF.Exp,
                                                                         bias=negC[s2][:, 0:1], scale=ATT_SCALE),
                      reads=[bb, b_negC[s2]], writes=[b_PT[s2][u]])
                if not use_mm_mask:
                    if blk == 0:
                        fn = lambda e, u=u: e.affine_select(out=PT[s2][u][:], in_=PT[s2][u][:], pattern=[[0, 4], [-1, 128]],
                                                            compare_op=ALU.is_gt, fill=0.0, base=0, channel_multiplier=1)
                    else:
                        fn = lambda e, u=u: e.affine_select(out=PT[s2][u][:], in_=PT[s2][u][:], pattern=[[0, 4], [1, 128]],
                                                            compare_op=ALU.is_ge, fill=0.0, base=0, channel_multiplier=-1)
                    S.add("pool", fn, reads=[b_PT[s2][u]], writes=[b_PT[s2][u]], cost={"pool": 0.6})

    def st_scT(c):
        s2 = c % 2
        bank, bb = alloc("B")
        for h in range(4):
            j, r = h // 2, h % 2
            S.add("pe", lambda e, h=h, j=j, r=r: e.matmul(bank[:, h * 128:(h + 1) * 128], lhsT=kTp[s2][r][:, j, :],
                                                          rhs=qT[s2][:, j, :], start=True, stop=True),
                  reads=[b_kTp[s2], b_qT[s2]], writes=[bb])
        S.add("dve", lambda e: e.tensor_tensor(out=scD[s2][:], in0=bank[:, 0:512], in1=causal[:], op=ALU.mult),
              reads=[bb, b_constB], writes=[b_scD[s2]])

    def st_PV(c):
        s2 = c % 2
        sk_cur, sk_prev = c % NKV, (c - 1) % NKV
        pvb = []
        for g in range(2):
            bank, bb = alloc("B")
            for j in range(4):
                for blk in range(2):
                    u = g * 2 + blk
                    sk = sk_prev if blk == 0 else sk_cur
                    S.add("pe", lambda e, bank=bank, j=j, blk=blk, u=u, sk=sk, g=g: e.matmul(
                        bank[:, j * 65:(j + 1) * 65], lhsT=PT[s2][u][:, j * 128:(j + 1) * 128], rhs=Vaug[sk][:, g, :],
                        start=(blk == 0), stop=(blk == 1)),
                        reads=[b_PT[s2][u], b_Vaug[sk]], writes=[bb])
            pvb.append((bank, bb))
        for g, (bank, bb) in enumerate(pvb):
            v = bank[:, 0:260].rearrange("p (j d) -> p j d", j=4)
            S.add("dve", lambda e, g=g, v=v: e.tensor_tensor(out=den[s2][:, g * 4:(g + 1) * 4], in0=v[:, :, 64],
                                                            in1=esk[s2][:, g * 4:(g + 1) * 4], op=ALU.add),
                  reads=[bb, b_esk[s2]], writes=[b_den[s2]])
        S.add("dve", lambda e: e.reciprocal(out=den[s2][:, 8:16], in_=den[s2][:, 0:8]),
              reads=[b_den[s2]], writes=[b_den[s2]])
        for g, (bank, bb) in enumerate(pvb):
            v = bank[:, 0:260].rearrange("p (j d) -> p j d", j=4)
            S.add("dve", lambda e, g=g, v=v: e.tensor_tensor(
                out=tmpa[0][:, g * 256:(g + 1) * 256].rearrange("p (j d) -> p j d", j=4), in0=v[:, :, 0:64],
                in1=den[s2][:, 8 + g * 4:8 + (g + 1) * 4].unsqueeze(2).to_broadcast([128, 4, 64]), op=ALU.mult),
                reads=[bb, b_den[s2]], writes=[b_tmpa[0]])
        S.add(tt(mix[s2][:, 0:512], tmpa[0][:], gate[s2][:, 0:512], ALU.mult),
              reads=[b_tmpa[0], b_gate[s2]], writes=[b_mix[s2]])

    def st_KV(c):
        s2 = c % 2
        sn = (c + 1) % 2
        bank, bb = alloc("B")
        for j in range(2):
            S.add("pe", lambda e, j=j: e.matmul(bank[:, j * 256:(j + 1) * 256],
                                                lhsT=qk[c % 3][:, 256 + j * 128:256 + (j + 1) * 128],
                                                rhs=rv[c % 3][:, j * 256:(j + 1) * 256], start=True, stop=True),
                  reads=[b_qk[c % 3], b_rv[c % 3]], writes=[bb])
        kv = bank[:, 0:512].rearrange("p (j x) -> p j x", j=2)
        for r in range(2):
            S.add("dve", lambda e, r=r: e.tensor_tensor(out=Sst[r * 64:(r + 1) * 64, :, :],
                                                        in0=Sst[r * 64:(r + 1) * 64, :, :],
                                                        in1=kv[r * 64:(r + 1) * 64, :, r * 128:(r + 1) * 128],
                                                        op=ALU.add),
                  reads=[bb, b_S], writes=[b_S])
        S.add(tt(Sst[:].rearrange("p a b -> p (a b)"), Sst[:].rearrange("p a b -> p (a b)"), cdtab[:], ALU.mult),
              reads=[b_S, b_const], writes=[b_S])
        if is_main(c) or c == NP - 1:
            for r in range(2):
                S.add(cp(Sbfp[sn][r][r * 64:(r + 1) * 64, :, :], Sst[r * 64:(r + 1) * 64, :, :], ("act", "dve", "pool")),
                      reads=[b_S], writes=[b_Sbfp[sn]])

    def st_OR(c):
        s2 = c % 2
        bank, bb = alloc("B")
        for h in range(4):
            j, r = h // 2, h % 2
            S.add("pe", lambda e, h=h: e.matmul(bank[:, h * 128:(h + 1) * 128], lhsT=scD[s2][:, h * 128:(h + 1) * 128],
                                                rhs=rv[c % 3][:, h * 128:(h + 1) * 128], start=True, stop=False),
                  reads=[b_scD[s2], b_rv[c % 3]], writes=[bb])
            S.add("pe", lambda e, h=h, j=j, r=r: e.matmul(bank[:, h * 128:(h + 1) * 128], lhsT=qT[s2][:, j, :],
                                                          rhs=Sbfp[s2][r][:, j, :], start=False, stop=True),
                  reads=[b_qT[s2], b_Sbfp[s2]], writes=[bb])
        G = gns[s2]
        S.add(cp(orsb[0][:], bank[:, 0:512]), reads=[bb], writes=[b_orsb[0]])
        ov = orsb[0][:].rearrange("p (h e) -> p h e", h=4)
        for h in range(4):
            S.add("dve", lambda e, h=h: e.bn_stats(out=G[:, h * 6:(h + 1) * 6], in_=orsb[0][:, h * 128:(h + 1) * 128]),
                  reads=[b_orsb[0]], writes=[b_gns[s2]], cost={"dve": 0.2})
        for h in range(4):
            S.add("dve", lambda e, h=h: e.bn_aggr(out=G[:, 24 + 2 * h:26 + 2 * h], in_=G[:, h * 6:(h + 1) * 6]),
                  reads=[b_gns[s2]], writes=[b_gns[s2]], cost={"dve": 0.15})
        S.add("dve", lambda e: e.tensor_scalar(out=G[:, 32:36],
                                               in0=G[:, 24:32].rearrange("p (h t) -> p h t", t=2)[:, :, 1],
                                               scalar1=GN_EPS, scalar2=None, op0=ALU.add),
              reads=[b_gns[s2]], writes=[b_gns[s2]])
        S.add("pool", lambda e: e.tensor_tensor(out=G[:, 36:40], in0=G[:, 32:36], in1=neghalf[:, 0:4], op=ALU.pow),
              reads=[b_gns[s2], b_misc], writes=[b_gns[s2]])
        S.add(tt(gg[0][:], gate[s2][:, 512:1024], gngh[:], ALU.mult),
              reads=[b_gate[s2], b_gngh], writes=[b_gg[0]])
        S.add(tt(gg[0][:].rearrange("p (h e) -> p h e", h=4), gg[0][:].rearrange("p (h e) -> p h e", h=4),
                 G[:, 36:40].unsqueeze(2).to_broadcast([128, 4, 128]), ALU.mult),
              reads=[b_gg[0], b_gns[s2]], writes=[b_gg[0]])
        for h in range(4):
            S.add("dve", lambda e, h=h: e.scalar_tensor_tensor(
                out=mix[s2][:, 512 + h * 128:512 + (h + 1) * 128], in0=orsb[0][:, h * 128:(h + 1) * 128],
                scalar=G[:, 24 + 2 * h:25 + 2 * h], in1=gg[0][:, h * 128:(h + 1) * 128], op0=ALU.subtract, op1=ALU.mult),
                reads=[b_orsb[0], b_gns[s2], b_gg[0]], writes=[b_mix[s2]])

    def st_M(c):
        s2 = c % 2
        bank, bb = alloc("A")
        pb = bank[:].bitcast(BF16)
        for kb in range(8):
            S.add("pe", lambda e, kb=kb: e.transpose(pb[:, kb * 128:(kb + 1) * 128],
                                                     mix[s2][:, kb * 128:(kb + 1) * 128], ident_bf[:]),
                  reads=[b_mix[s2], b_const], writes=[bb])
        S.add(cp(mixT[s2][:], pb[:, 0:1024]), reads=[bb], writes=[b_mixT[s2]])

    def st_O(c):
        s2, sx = c % 2, c % NX
        m = c - NP
        for half in range(2):
            bank, bb = alloc("A")
            for kb in range(8):
                S.add("pe", lambda e, kb=kb, bank=bank, half=half: e.matmul(
                    bank[:, 0:512], lhsT=mixT[s2][:, kb * 128:(kb + 1) * 128],
                    rhs=Wo[:, kb, half * 512:(half + 1) * 512], start=(kb == 0), stop=(kb == 7)),
                    reads=[b_mixT[s2]] + b_Wo, writes=[bb])
            S.add("dve", lambda e, bank=bank, half=half: e.tensor_tensor(
                out=yy[s2][:, half * 512:(half + 1) * 512], in0=bank[:, 0:512],
                in1=xs[sx][:, half * 512:(half + 1) * 512], op=ALU.add),
                reads=[bb, b_xs[sx]], writes=[b_yy[s2]])
        st = stat2[s2]
        S.add("act", lambda e: e.activation(out=junk[:], in_=yy[s2][:], func=AF.Square, accum_out=st[:, 0:1]),
              reads=[b_yy[s2]], writes=[b_junk, b_stat2[s2]])
        S.add("dve", lambda e: e.tensor_scalar(out=st[:, 1:2], in0=st[:, 0:1], scalar1=1.0 / D, scalar2=RMS_EPS,
                                               op0=ALU.mult, op1=ALU.add),
              reads=[b_stat2[s2]], writes=[b_stat2[s2]])
        S.add("pool", lambda e: e.tensor_tensor(out=st[:, 2:3], in0=st[:, 1:2], in1=neghalf[:, 0:1], op=ALU.pow),
              reads=[b_stat2[s2], b_misc], writes=[b_stat2[s2]])
        S.add("dve", lambda e: e.scalar_tensor_tensor(out=yy[s2][:], in0=yy[s2][:], scalar=st[:, 2:3], in1=fg_rep[:],
                                                      op0=ALU.mult, op1=ALU.mult),
              reads=[b_yy[s2], b_stat2[s2], b_constB], writes=[b_yy[s2]])
        S.add("sp", dma(out_d[m * CH:(m + 1) * CH, :], yy[s2][:]), reads=[b_yy[s2]], dmaq=f"st{s2}")

    def run_stage(name, c):
        if c < 0 or c >= NT:
            return
        main = is_main(c)
        S.cur_tag = name + ('' if main else '_pre')
        S.cur_chunk = c
        if name == "load":
            st_load(c)
        elif name == "front":
            st_front(c)
        elif name == "T":
            st_T(c)
        elif name == "P":
            if main:
                st_P_main(c, 1)
            else:
                st_P_prefix(c)
        elif name == "P2":
            if main:
                st_P_main(c, 2)
        elif name == "KV":
            st_KV(c)
        elif not main:
            return
        elif name == "R":
            st_R(c)
        elif name == "SC":
            st_SC(c)
        elif name == "scT":
            st_scT(c)
        elif name == "PV":
            st_PV(c)
        elif name == "OR":
            st_OR(c)
        elif name == "M":
            st_M(c)
        elif name == "O":
            st_O(c)

    order = opts["order"]
    t_first = -4 - opts["pre_lead"]
    for t in range(t_first, NT + 3):
        if t == t_first + 2:
            for task in early_w:
                load_weight(*task)
            emit_late_consts()
            emit_gngh()
        if t >= 0 and pending_w:
            for _ in range(max(1, -(-24 // max(1, NP - 1)))):
                if pending_w:
                    load_weight(*pending_w.pop(0))
        for name, off in order:
            if name in ("load", "front"):
                cp_ = t - off + opts["pre_lead"]
                if 0 <= cp_ < NP:
                    run_stage(name, cp_)
                cm_ = t - off
                if cm_ >= NP:
                    run_stage(name, cm_)
            else:
                run_stage(name, t - off)

    if SCHEDULE:
        S.schedule(window=opts["window"], urgency=opts["urgency"], bin_us=opts["bin_us"], bias=opts["bias"], bias_pre=opts["bias_pre"])
        build_program.last_makespan = S.makespan
    S.finalize()
    S.emit(nc, block, sems, [f"st{i}" for i in range(2)])
    es.close()
    return nc


def _perm_ret_qk():
    idx = []
    for h in range(4):
        base = h * 64
        idx += [base + 2 * i for i in range(32)] + [base + 2 * i + 1 for i in range(32)]
    return np.array(idx)


def _const_tables(pos0_list, first_half):
    theta = (1.0 / (10000.0 ** np.linspace(0.0, 1.0, 32, dtype=np.float32))).astype(np.float32)
    gam = 1.0 - 2.0 ** (-5.0 - np.arange(4, dtype=np.float64))
    i = np.arange(128, dtype=np.float64)
    dq = gam[None, :] ** (i[:, None] + 1.0)
    dk = gam[None, :] ** (-(i[:, None] + 1.0)) / 8.0
    tabs = np.zeros((len(pos0_list), 128, 4, 4, 32), np.float32)
    for n, p0 in enumerate(pos0_list):
        pos = (p0 + np.arange(128)).astype(np.float32)
        ang = pos[:, None] * theta[None, :]
        c, s = np.cos(ang).astype(np.float64), np.sin(ang).astype(np.float64)
        tabs[n, :, 0] = c[:, None, :] * dq[:, :, None]
        tabs[n, :, 1] = c[:, None, :] * dk[:, :, None]
        tabs[n, :, 2] = s[:, None, :] * dq[:, :, None]
        tabs[n, :, 3] = s[:, None, :] * dk[:, :, None]
    return tabs.reshape(len(pos0_list) * 128, 512)


def _static_consts():
    j = np.arange(128)[:, None]
    i = np.arange(128)[None, :]
    mp = np.where(j > i, 0.0, NEG).astype(np.float32)
    mc = np.where(j <= i, 0.0, NEG).astype(np.float32)
    mp0 = np.full((128, 128), NEG, np.float32)
    t4 = lambda a: np.tile(a, (1, 4))
    causal = t4(np.where(j <= i, 1.0, 0.0).astype(np.float32))
    gam = 1.0 - 2.0 ** (-5.0 - np.arange(4, dtype=np.float64))
    cd = gam ** 128.0
    cdtab = np.zeros((128, 2, 128), np.float32)
    for r in range(2):
        for jj in range(2):
            cdtab[r * 64:(r + 1) * 64, jj, :] = cd[2 * jj + r]
    gsel = np.zeros((128, 2, 128), np.float32)
    gsel[0:64, 0, :] = 1.0
    gsel[64:128, 1, :] = 1.0
    return dict(mp0=t4(mp0), mp=t4(mp), mc=t4(mc), causal=causal, cdtab=cdtab.reshape(128, 256),
                ident=np.eye(128, dtype=np.float32), gsel=gsel.reshape(128, 256))


def _prep_weights(w_in, w_out):
    w = np.asarray(w_in, np.float32)[0]
    cuts = np.cumsum([512, 128, 128, 512, 256, 256, 512, 512])[:-1]
    aq, ak, av, az, rq, rk, rv, rz = np.split(w, cuts, axis=1)
    p = _perm_ret_qk()
    rq, rk = rq[:, p], rk[:, p]
    aqp = np.concatenate([np.concatenate([aq[:, j * 64:(j + 1) * 64], aq[:, (j + 4) * 64:(j + 5) * 64]], 1)
                          for j in range(4)], 1)
    wa = np.ascontiguousarray(np.concatenate([ak, av, rq, rk, rv], 1))
    wb = np.ascontiguousarray(np.concatenate([aqp, az, rz], 1))
    wo = np.ascontiguousarray(np.asarray(w_out, np.float32)[0])
    return wa, wb, wo


def _core_inputs(xb, first_half, NP, NM, pos_main0, shared):
    main = xb[pos_main0:pos_main0 + NM * CH]
    if first_half:
        pre = np.zeros((NP * CH, D), np.float32)
        pos_pre0 = 0
    else:
        pre = xb[pos_main0 - NP * CH:pos_main0]
        pos_pre0 = pos_main0 - NP * CH
    pos0 = [pos_pre0 + n * CH for n in range(NP)] + [pos_main0 + n * CH for n in range(NM)]
    sc = shared["sc"]
    masks = np.concatenate([sc["mp0"] if first_half else sc["mp"], sc["mp"], sc["mc"]], 1)
    d = dict(shared["common"])
    d["x_all"] = np.ascontiguousarray(np.concatenate([pre, main], 0), dtype=np.float32)
    d["tabs"] = _const_tables(pos0, first_half)
    d["masks"] = masks.astype(ml_dtypes.bfloat16)
    return d


def _shared(norm_g, w_in, att_sinks, ret_gn_g, w_out, final_g):
    sc = _static_consts()
    wa, wb, wo = _prep_weights(w_in, w_out)
    rep = lambda v: np.ascontiguousarray(np.broadcast_to(np.asarray(v, np.float32).reshape(1, -1), (128, np.asarray(v).size)))
    common = dict(w_a=wa, w_b=wb, w_o=wo, g_col=np.ascontiguousarray(np.asarray(norm_g, np.float32).reshape(8, 128).T), fg_rep=rep(final_g), gng_rep=rep(ret_gn_g),
                  sinks_rep=rep(att_sinks), causal01=sc["causal"], cdtab=sc["cdtab"],
                  ident_bf=sc["ident"].astype(ml_dtypes.bfloat16), gsel=sc["gsel"].astype(ml_dtypes.bfloat16))
    return dict(sc=sc, common=common)


_PROG = {}


def kernel(x, norm_g, w_in, att_sinks, ret_gn_g, w_out, final_g):
    x = np.asarray(x, np.float32)
    B = x.shape[0]
    NP = NM = HALF // CH
    shared = _shared(norm_g, w_in, att_sinks, ret_gn_g, w_out, final_g)
    in_maps = []
    for c in range(NCORES):
        b, half = c // 2, c % 2
        in_maps.append(_core_inputs(x[b], half == 0, NP, NM, half * HALF, shared))
    if "nc" not in _PROG:
        _PROG["nc"] = build_program(NP, NM)
    res = run_bass_kernel_spmd(_PROG["nc"], in_maps, core_ids=list(range(NCORES)))
    out = np.empty((B, SEQ, D), np.float32)
    for c in range(NCORES):
        b, half = c // 2, c % 2
        out[b, half * HALF:(half + 1) * HALF] = res.results[c]["out"]
    return out
```

```python
import math
from contextlib import ExitStack

import numpy as np
import ml_dtypes

import concourse.bass as bass
import concourse.mybir as mybir
from concourse.bass_utils import run_bass_kernel_spmd

F32 = mybir.dt.float32
BF16 = mybir.dt.bfloat16
AF = mybir.ActivationFunctionType
ALU = mybir.AluOpType
AX = mybir.AxisListType

D = 1024
NCORES = 8
SEQ = 8192
HALF = SEQ // 2
CH = 128
RMS_EPS = 1e-6
GN_EPS = 1e-6
NEG = -30000.0
ATT_SCALE = 64 ** -0.5
SCHEDULE = True
DEFAULT_OPTS = dict(
    pools={"A": [0, 1, 2], "B": [3, 4, 5, 6]},
    order=[("load", -4), ("M", 2), ("scT", 1), ("PV", 1), ("OR", 1), ("KV", 1), ("O", 2), ("R", 0), ("SC", 0),
           ("T", -2), ("P", -1), ("P2", -1), ("front", -3)],
    pre_lead=1,
    window=64,
    urgency=True,
    bin_us=0.5,
    bias={"dve": 1.5},
    bias_pre={},
    hscale_main=("act", "dve", "pool"),
    hscale_pre=("dve",),
    use_pool=False,
    mask_pool=False,
)

A_AK, A_AV, A_RQ, A_RK, A_RV = 0, 128, 256, 512, 768
B_AQ, B_AZ, B_RZ = 0, 512, 1024
WA = 1280
WB = 1536


class Buf:
    __slots__ = ("name", "last_w", "readers", "excl", "owner")

    def __init__(self, name, excl=False):
        self.owner = None
        self.name = name
        self.last_w = None
        self.readers = []
        self.excl = excl


class Op:
    __slots__ = ("eng", "fn", "deps", "signals", "val", "queue", "inc", "wait_all", "idx", "cost", "t0", "t1",
                 "urg", "alts", "is_dma", "tag")

    def __init__(self, alts, queue, inc, wait_all, is_dma):
        self.alts = alts
        self.eng = next(iter(alts)) if len(alts) == 1 else None
        self.fn = alts[self.eng] if self.eng else None
        self.deps = []
        self.signals = False
        self.val = None
        self.queue = queue
        self.inc = inc
        self.wait_all = wait_all
        self.is_dma = is_dma
        self.idx = 0
        self.cost = {}
        self.t0 = None
        self.t1 = None
        self.urg = 0


class Sched:
    ENGS = ("pe", "act", "dve", "pool", "sp")
    CROSS_CHUNK = ("kTa", "Vaug", "Sbfp", "S", "Mk", "const", "misc", "W", "wstage", "c", "junk")

    def __init__(self):
        self.cur_chunk = None
        self.streams = {e: [] for e in self.ENGS}
        self.queues = {}
        self.n_ops = 0
        self.cur_tag = 'init'

    def add(self, eng, fn=None, reads=(), writes=(), dmaq=None, wait_all=False, cost=None):
        if isinstance(eng, str):
            alts = {eng: fn}
        elif isinstance(eng, dict):
            alts = dict(eng)
        else:
            alts = {e: fn for e in eng}
        if dmaq is None:
            op = Op(alts, None, 1, False, False)
        else:
            op = Op(alts, dmaq, 16, wait_all, True)
        deps = []
        seen = set()
        is_pe = op.eng == "pe"

        def dep(o):
            if o is None or o is op or id(o) in seen:
                return
            if is_pe and o.eng == "pe" and not o.is_dma and dmaq is None:
                return
            seen.add(id(o))
            deps.append(o)

        for b in reads:
            dep(b.last_w)
            if b.excl:
                for r in b.readers:
                    if r.eng is None or op.eng is None or r.eng != op.eng:
                        dep(r)
        for b in writes:
            dep(b.last_w)
            for r in b.readers:
                dep(r)
        for b in reads:
            b.readers.append(op)
            if (b.owner is not None and self.cur_chunk is not None and b.owner != self.cur_chunk
                    and not b.name.startswith(self.CROSS_CHUNK)):
                raise AssertionError(f"slot {b.name} holds chunk {b.owner} but stage {self.cur_tag} of chunk "
                                     f"{self.cur_chunk} reads it")
        for b in writes:
            b.last_w = op
            b.readers = []
            b.owner = self.cur_chunk
        for o in deps:
            o.signals = True
        op.deps = deps
        if dmaq is not None:
            op.signals = True
        op.idx = self.n_ops
        op.tag = self.cur_tag
        if cost is not None:
            op.cost = dict(cost)
        self.n_ops += 1
        for e in alts:
            self.streams[e].append(op)
        return op

    def estimate_costs(self):
        class Rec:
            def __init__(self):
                self.name = None
                self.args = None
                self.kw = None

            def __getattr__(self, name):
                def f(*a, **kw):
                    self.name, self.args, self.kw = name, a, kw
                    return None
                return f

        def free_size(ap):
            n = 1
            for d in ap.shape[1:]:
                n *= d
            return n

        seen = set()
        for eng, stream in self.streams.items():
            for op in stream:
                if id(op) in seen:
                    continue
                seen.add(id(op))
                for e, fn in op.alts.items():
                    if e in op.cost:
                        continue
                    r = Rec()
                    fn(r)
                    kw, a = r.kw, r.args
                    if e == "pe":
                        n = free_size(kw["rhs"]) if r.name == "matmul" else free_size(a[1])
                        c = max(0.058, n / 2160.0) + (0.024 if n == 256 else 0.0)
                    elif e == "sp":
                        c = 0.1
                    else:
                        out = kw.get("out", a[0] if a else None)
                        n = free_size(out) if out is not None else 128
                        if e == "act":
                            c = 0.22 + n * 0.0009
                        elif e == "dve":
                            c = 0.13 + n * 0.0011
                        else:
                            c = 0.45 + n * 0.0017
                    op.cost[e] = c

    def schedule(self, window=40, lat=0.2, dma_lat=4.5, urgency=True, bin_us=0.5, bias=None, bias_pre=None):
        self.bin_us = bin_us
        self.bias = bias or {}
        self.bias_pre = bias_pre or {}
        self.estimate_costs()
        allops = {}
        for st_ in self.streams.values():
            for o in st_:
                allops[id(o)] = o
        allops = sorted(allops.values(), key=lambda o: o.idx)
        INF = 1 << 60
        for o in allops:
            o.urg = o.idx if o.eng == "pe" else INF
        for o in reversed(allops):
            for d in o.deps:
                if o.urg < d.urg:
                    d.urg = o.urg
        if not urgency:
            for o in allops:
                o.urg = o.idx
        ptr = {e: 0 for e in self.ENGS}
        free_at = {e: 0.0 for e in self.ENGS}
        remaining = len(allops)

        def ready_time(op):
            t = 0.0
            for d in op.deps:
                if d.t1 is None:
                    return None
                t = max(t, d.t1 + lat)
            return t

        BIN = self.bin_us
        while remaining:
            best = None
            cand = {}
            for e in self.ENGS:
                stream = self.streams[e]
                n = len(stream)
                p = ptr[e]
                while p < n and stream[p].t0 is not None:
                    p += 1
                ptr[e] = p
                if p >= n:
                    continue
                w = 1 if e in ("pe", "sp") else window
                cnt = 0
                i = p
                while i < n and cnt < w:
                    op = stream[i]
                    if op.t0 is None:
                        cnt += 1
                        rt = ready_time(op)
                        if rt is not None:
                            st = max(rt, free_at[e])
                            bias = self.bias_pre if op.tag.endswith("_pre") else self.bias
                            fin = st + op.cost[e] + (bias.get(e, 0.0) if len(op.alts) > 1 else 0.0)
                            c = cand.get(id(op))
                            if c is None or fin < c[0]:
                                cand[id(op)] = (fin, st, e, op)
                    i += 1
            for fin, st, e, op in cand.values():
                key = (int(st / BIN), op.urg, st, op.idx)
                if best is None or key < best[0]:
                    best = (key, e, op, st)
            assert best is not None, "scheduler deadlock"
            _, e, op, st = best
            op.t0 = st
            op.eng = e
            op.fn = op.alts[e]
            if e == "sp":
                free_at[e] = st + op.cost[e]
                op.t1 = st + dma_lat
            else:
                op.t1 = st + op.cost[e]
                free_at[e] = op.t1
            remaining -= 1
        for e in self.ENGS:
            self.streams[e] = sorted((o for o in self.streams[e] if o.eng == e), key=lambda o: (o.t0, o.idx))
        self.makespan = max(o.t1 for o in allops)

    def finalize(self):
        self.totals = {}
        self.queues = {}
        for e in self.ENGS:
            for op in self.streams[e]:
                if op.eng is None:
                    op.eng = e
                    op.fn = op.alts[e]
                if op.eng != e:
                    continue
                if not op.is_dma:
                    op.queue = e
                self.queues.setdefault(op.queue, []).append(op)
            self.streams[e] = [o for o in self.streams[e] if o.eng == e]
        for q, ops in self.queues.items():
            n = 0
            for op in ops:
                if op.signals:
                    n += op.inc
                    op.val = n
            self.totals[q] = n
            for op in ops:
                if op.wait_all:
                    op.val = n

    def emit(self, nc, block, sems, final_waits):
        engmap = {"pe": block.tensor, "act": block.scalar, "dve": block.vector,
                  "pool": block.gpsimd, "sp": block.sync}
        for eng in self.ENGS:
            stream = self.streams[eng]

            def body(e, stream=stream, eng=eng):
                waited = {}
                for op in stream:
                    for d in op.deps:
                        if waited.get(d.queue, 0) < d.val:
                            e.wait_ge(sems[d.queue], d.val)
                            waited[d.queue] = d.val
                    ins = op.fn(e)
                    if op.signals:
                        ins.then_inc(sems[op.queue], op.inc)
                if eng == "sp":
                    for q in final_waits:
                        if self.totals.get(q, 0) > 0:
                            e.wait_ge(sems[q], self.totals[q])

            engmap[eng](body)


def build_program(NP, NM, opts=None):
    NT = NP + NM
    opts = dict(DEFAULT_OPTS, **(opts or {}))
    nc = bass.Bass("TRN2", target_bir_lowering=False)
    S = Sched()
    es = ExitStack()

    def dram(name, shape, dt, kind="ExternalInput"):
        return nc.dram_tensor(name, list(shape), dt, kind=kind).ap()

    x_d = dram("x_all", [NT * CH, D], F32)
    tab_d = dram("tabs", [NT * CH, 512], F32)
    wa_d = dram("w_a", [D, WA], F32)
    wb_d = dram("w_b", [D, WB], F32)
    wo_d = dram("w_o", [D, D], F32)
    g_d = dram("g_col", [128, 8], F32)
    fg_d = dram("fg_rep", [128, D], F32)
    gng_d = dram("gng_rep", [128, 512], F32)
    snk_d = dram("sinks_rep", [128, 8], F32)
    msk_d = dram("masks", [128, 3 * 512], BF16)
    cz_d = dram("causal01", [128, 512], F32)
    cd_d = dram("cdtab", [128, 256], F32)
    idb_d = dram("ident_bf", [128, 128], BF16)
    gs_d = dram("gsel", [128, 256], BF16)
    out_d = dram("out", [NM * CH, D], F32, kind="ExternalOutput")

    def sb(name, shape, dt):
        return es.enter_context(nc.sbuf_tensor(name, list(shape), dt))

    Wa = sb("Wa", [128, 8, WA], BF16)
    Wb = sb("Wb", [128, 8, WB], BF16)
    Wo = sb("Wo", [128, 8, D], BF16)
    g_col = sb("g_col_s", [128, 8], F32)
    gngh = sb("gngh", [128, 512], F32)
    fg_rep = sb("fg_rep_s", [128, D], F32)
    gng_rep = sb("gng_rep_s", [128, 512], F32)
    sinks = sb("sinks_s", [128, 8], F32)
    negsinkmax = sb("negsinkmax", [128, 1], F32)
    masks = sb("masks_s", [128, 3 * 512], BF16)
    causal = sb("causal_s", [128, 512], F32)
    cdtab = sb("cdtab_s", [128, 256], F32)
    ident_bf = sb("ident_bf_s", [128, 128], BF16)
    gsel = sb("gsel_s", [128, 256], BF16)
    chalf = sb("chalf", [128, 8], F32)
    neghalf = sb("neghalf", [128, 8], F32)
    cst = sb("cst", [128, 8], F32)
    sinkmax = sb("sinkmax", [128, 1], F32)
    ones_bf = sb("ones_bf", [128, 128], BF16)
    sinks2 = sb("sinks2", [128, 8], F32)
    b_const = Buf("const")
    b_Wa_kv, b_Wa_rest, b_Wb, b_Wo = ([Buf(f"{n}{k}") for k in range(8)] for n in ("Wakv", "War", "Wb", "Wo"))
    b_Wa = b_Wa_kv + b_Wa_rest

    NST = 3
    wstage = [sb(f"wstage{i}", [128, max(WA, WB)], F32) for i in range(NST)]
    b_wstage = [Buf(f"wstage{i}") for i in range(NST)]

    def slots(name, n, shape, dt):
        return [sb(f"{name}{i}", shape, dt) for i in range(n)], [Buf(f"{name}{i}") for i in range(n)]

    NX = 7
    xs, b_xs = slots("xs", NX, [128, D], F32)
    NTB = 5
    tb, b_tb = slots("tb", NTB, [128, 512], F32)
    junk = sb("junk", [128, D], BF16)
    b_junk = Buf("junk")
    stat, b_stat = slots("stat", 4, [128, 4], F32)
    hh, b_hh = slots("hh", 2, [128, D], BF16)
    hT, b_hT = slots("hT", 2, [128, D], BF16)
    aqT, b_aqT = slots("aqT", 2, [128, 512], BF16)
    NKV = 4
    kTa = [[sb(f"kTa{i}_{g}", [128, 128], BF16) for g in range(2)] for i in range(NKV)]
    b_kTa = [Buf(f"kTa{i}") for i in range(NKV)]
    Vaug, b_Vaug = slots("Vaug", NKV, [128, 2, 65], BF16)
    rotu, b_rotu = slots("rotu", 1, [128, 512], F32)
    rotw, b_rotw = slots("rotw", 1, [128, 512], F32)
    qk, b_qk = slots("qk", 3, [128, 512], BF16)
    qT, b_qT = slots("qT", 2, [128, 2, 128], BF16)
    kTp = [[sb(f"kTp{i}_{r}", [128, 2, 128], BF16) for r in range(2)] for i in range(2)]
    b_kTp = [Buf(f"kTp{i}") for i in range(2)]
    rv, b_rv = slots("rv", 3, [128, 512], BF16)
    gate, b_gate = slots("gate", 2, [128, 1024], F32)
    PT = [[sb(f"PT{i}_{u}", [128, 512], BF16) for u in range(4)] for i in range(2)]
    b_PT = [[Buf(f"PT{i}_{u}") for u in range(4)] for i in range(2)]
    sqq, b_sqq = slots("sqq", 1, [128, 512], BF16)
    sqk, b_sqk = slots("sqk", 1, [128, 128], BF16)
    Mq, b_Mq = slots("Mq", 2, [128, 2], F32)
    Mk, b_Mk = slots("Mk", 3, [128, 2], F32)
    tmpc, b_tmpc = slots("tmpc", 2, [128, 8], F32)
    negC, b_negC = slots("negC", 2, [128, 2], F32)
    orsb, b_orsb = slots("orsb", 1, [128, 512], F32)
    esk, b_esk = slots("esk", 2, [128, 8], F32)
    scD, b_scD = slots("scD", 2, [128, 512], BF16)
    Sst = sb("Sst", [128, 2, 128], F32)
    b_S = Buf("S")
    Sbfp = [[sb(f"Sbfp{i}_{r}", [128, 2, 128], BF16) for r in range(2)] for i in range(2)]
    b_Sbfp = [Buf(f"Sbfp{i}") for i in range(2)]
    den, b_den = slots("den", 2, [128, 16], F32)
    tmpa, b_tmpa = slots("tmpa", 1, [128, 512], F32)
    gns, b_gns = slots("gns", 2, [128, 48], F32)
    gg, b_gg = slots("gg", 1, [128, 512], F32)
    mix, b_mix = slots("mix", 2, [128, D], BF16)
    mixT, b_mixT = slots("mixT", 2, [128, D], BF16)
    yy, b_yy = slots("yy", 2, [128, D], F32)
    stat2, b_stat2 = slots("stat2", 2, [128, 4], F32)

    banks = [es.enter_context(nc.psum_tensor(f"ps{i}", [128, 512], F32)) for i in range(8)]
    b_bank = [Buf(f"bank{i}", excl=True) for i in range(8)]
    b_smallA, b_smallB = Buf("smallA", excl=True), Buf("smallB", excl=True)
    pools = {k: list(v) for k, v in opts["pools"].items()}
    rr = {"A": 0, "B": 0}

    def alloc(pool):
        lst = pools[pool]
        i = lst[rr[pool] % len(lst)]
        rr[pool] += 1
        return banks[i], b_bank[i]

    qnames = ["pe", "act", "dve", "pool", "const", "constB"]
    qnames += [f"ws{i}" for i in range(NST)]
    qnames += [f"xl{i}" for i in range(NX)] + [f"tl{i}" for i in range(5)] + [f"st{i}" for i in range(2)]
    sems = {q: es.enter_context(nc.semaphore(q)) for q in qnames}
    block = es.enter_context(nc.Block())

    def dma(eng_ap_out, eng_ap_in):
        return lambda e: e.dma_start(out=eng_ap_out, in_=eng_ap_in)

    def _flt(engines):
        if not opts["use_pool"]:
            e2 = tuple(x for x in engines if x != "pool")
            return e2 if e2 else engines
        return engines

    def cp(out, in_, engines=("act", "dve")):
        engines = _flt(engines)
        d = {}
        for en in engines:
            if en == "act":
                d[en] = lambda e: e.activation(out=out, in_=in_, func=AF.Copy)
            else:
                d[en] = lambda e: e.tensor_copy(out=out, in_=in_)
        return d

    def tt(out, in0, in1, op, engines=("dve", "pool")):
        engines = _flt(engines)
        return {en: (lambda e: e.tensor_tensor(out=out, in0=in0, in1=in1, op=op)) for en in engines}

    def scale_rows(out, in_, col, n, engines=("act", "dve", "pool")):
        engines = _flt(engines)
        d = {}
        for en in engines:
            if en == "act":
                d[en] = lambda e: e.activation(out=out, in_=in_, func=AF.Copy, scale=col)
            elif en == "dve":
                d[en] = lambda e: e.tensor_scalar(out=out, in0=in_, scalar1=col, scalar2=None, op0=ALU.mult)
            else:
                d[en] = lambda e: e.tensor_tensor(out=out, in0=in_, in1=col.to_broadcast([128, n]), op=ALU.mult)
        return d

    consts = [(ident_bf, idb_d), (g_col, g_d), (sinks, snk_d), (cdtab, cd_d)]
    consts_late = [(masks, msk_d), (causal, cz_d), (gng_rep, gng_d), (fg_rep, fg_d), (gsel, gs_d)]
    for t, d_ in consts:
        last_const = S.add("sp", dma(t[:], d_[:, :]), writes=[Buf("c")], dmaq="const", wait_all=True)
    b_const.last_w = last_const
    last_const.signals = True
    b_constB = Buf("constB")

    def emit_late_consts():
        for t, d_ in consts_late:
            lc = S.add("sp", dma(t[:], d_[:, :]), writes=[Buf("c")], dmaq="constB", wait_all=True)
        b_constB.last_w = lc
        lc.signals = True
    b_misc = Buf("misc")
    b_gngh = Buf("misc_gngh")
    S.add("pool", lambda e: e.memset(neghalf[:], -0.5), writes=[b_misc])
    S.add("pool", lambda e: e.memset(chalf[:], 0.5), writes=[b_misc])
    S.add("pool", lambda e: e.memset(cst[:, 0:1], -ATT_SCALE), writes=[b_misc])
    S.add("pool", lambda e: e.memset(cst[:, 1:2], -1.0), writes=[b_misc])
    S.add("pool", lambda e: e.memset(ones_bf[:], 1.0), writes=[b_misc])
    S.add("pool", lambda e: e.memset(cst[:, 2:3], -0.5 * ATT_SCALE), writes=[b_misc])

    def memset_all(t, val, buf, eng="pool"):
        S.add(eng, lambda e, t=t, val=val: e.memset(t[:], val), writes=[buf])

    for i in range(NKV):
        for g in range(2):
            memset_all(kTa[i][g], 0.0, b_kTa[i])
        memset_all(Vaug[i], 2.0, b_Vaug[i])
    for i in range(2):
        for r in range(2):
            memset_all(kTp[i][r], 0.0, b_kTp[i])
            memset_all(Sbfp[i][r], 0.0, b_Sbfp[i])
    memset_all(Sst, 0.0, b_S)
    S.add("dve", lambda e: e.tensor_scalar(out=sinks2[:], in0=sinks[:], scalar1=math.log(2.0), scalar2=None, op0=ALU.add),
          reads=[b_const], writes=[b_misc])
    S.add("dve", lambda e: e.tensor_reduce(out=sinkmax[:], in_=sinks[:], axis=AX.X, op=ALU.max),
          reads=[b_const], writes=[b_misc])
    S.add("dve", lambda e: e.tensor_scalar(out=negsinkmax[:], in0=sinkmax[:], scalar1=-1.0, scalar2=None,
                                           op0=ALU.mult),
          reads=[b_misc], writes=[b_misc])

    def emit_gngh():
        S.add("dve", lambda e: e.tensor_scalar(out=gngh[:], in0=gng_rep[:], scalar1=0.5, scalar2=None, op0=ALU.mult),
              reads=[b_constB], writes=[b_gngh])

    cast_rr = [0]

    def load_weight(dst, b_dst, src_d, c0, c1, kb, scaled):
        width = c1 - c0
        si = cast_rr[0] % NST
        cast_rr[0] += 1
        st_t, b_st = wstage[si], b_wstage[si]
        S.add("sp", lambda e: e.dma_start(out=st_t[:, 0:width], in_=src_d[kb * 128:(kb + 1) * 128, c0:c1]),
              writes=[b_st], dmaq=f"ws{si}")
        if scaled:
            alts = scale_rows(dst[:, kb, c0:c1], st_t[:, 0:width], g_col[:, kb:kb + 1], width)
        else:
            alts = cp(dst[:, kb, c0:c1], st_t[:, 0:width], ("act", "dve", "pool"))
        S.add(alts, reads=[b_st, b_const], writes=[b_dst[kb]])

    def weight_tasks(dst, b_dst, src_d, c0, c1, scaled):
        return [(dst, b_dst, src_d, c0, c1, kb, scaled) for kb in range(8)]

    early_w = weight_tasks(Wa, b_Wa_kv, wa_d, A_RK, WA, True)
    pending_w = (weight_tasks(Wa, b_Wa_rest, wa_d, 0, A_RK, True) + weight_tasks(Wb, b_Wb, wb_d, 0, WB, True)
                 + weight_tasks(Wo, b_Wo, wo_d, 0, D, False))


    def is_main(c):
        return c >= NP

    def st_load(c):
        sx = c % NX
        S.add("sp", dma(xs[sx][:], x_d[c * CH:(c + 1) * CH, :]), writes=[b_xs[sx]], dmaq=f"xl{sx}")
        S.add("sp", dma(tb[c % NTB][:], tab_d[c * CH:(c + 1) * CH, :]), writes=[b_tb[c % NTB]], dmaq=f"tl{c % NTB}")

    def st_front(c):
        sx, st = c % NX, c % 4
        S.add("act", lambda e: e.activation(out=hh[c % 2][:], in_=xs[sx][:], func=AF.Square,
                                            accum_out=stat[st][:, 0:1]),
              reads=[b_xs[sx]], writes=[b_hh[c % 2], b_stat[st]])
        S.add("dve", lambda e: e.tensor_scalar(out=stat[st][:, 1:2], in0=stat[st][:, 0:1], scalar1=1.0 / D,
                                               scalar2=RMS_EPS, op0=ALU.mult, op1=ALU.add),
              reads=[b_stat[st]], writes=[b_stat[st]])
        S.add("pool", lambda e: e.tensor_tensor(out=stat[st][:, 2:3], in0=stat[st][:, 1:2],
                                                in1=neghalf[:, 0:1], op=ALU.pow),
              reads=[b_stat[st], b_misc], writes=[b_stat[st]])
        S.add(scale_rows(hh[c % 2][:], xs[sx][:], stat[st][:, 2:3], D,
                         engines=(opts["hscale_main"] if is_main(c) else opts["hscale_pre"])),
              reads=[b_xs[sx], b_stat[st]], writes=[b_hh[c % 2]])

    def st_T(c):
        bank, bb = alloc("A")
        pb = bank[:].bitcast(BF16)
        for kb in range(8):
            S.add("pe", lambda e, kb=kb: e.transpose(pb[:, kb * 128:(kb + 1) * 128],
                                                     hh[c % 2][:, kb * 128:(kb + 1) * 128], ident_bf[:]),
                  reads=[b_hh[c % 2], b_const], writes=[bb])
        S.add(cp(hT[c % 2][:], pb[:, 0:1024]), reads=[bb], writes=[b_hT[c % 2]])

    def mm_tok(bank, bb, c0, W, b_W, w0, n, c):
        for kb in range(8):
            S.add("pe", lambda e, kb=kb: e.matmul(bank[:, c0:c0 + n], lhsT=hT[c % 2][:, kb * 128:(kb + 1) * 128],
                                                  rhs=W[:, kb, w0:w0 + n], start=(kb == 0), stop=(kb == 7)),
                  reads=[b_hT[c % 2]] + b_W, writes=[bb])

    def mm_feat(bank, bb, c0, W, b_W, w0, c):
        for kb in range(8):
            S.add("pe", lambda e, kb=kb: e.matmul(bank[:, c0:c0 + 128], lhsT=W[:, kb, w0:w0 + 128],
                                                  rhs=hT[c % 2][:, kb * 128:(kb + 1) * 128],
                                                  start=(kb == 0), stop=(kb == 7)),
                  reads=[b_hT[c % 2]] + b_W, writes=[bb])

    def ev_kv(c, bank, bb):
        sk = c % NKV
        for g in range(2):
            S.add(cp(kTa[sk][g][g * 64:(g + 1) * 64, :], bank[g * 64:(g + 1) * 64, 0:128]),
                  reads=[bb], writes=[b_kTa[sk]])
        S.add(cp(Vaug[sk][:, :, 0:64], bank[:, 128:256].rearrange("p (g d) -> p g d", g=2)),
              reads=[bb], writes=[b_Vaug[sk]])

    def ev_rot(c, bank, bb, nqk):
        s2, t3, q3 = c % 2, c % NTB, c % 3
        if nqk == 2:
            src = bank[:, 0:512].rearrange("p (g t i) -> p g t i", g=8, t=2)
            cC = tb[t3][:, 0:256].rearrange("p (g i) -> p g i", g=8)
            cS = tb[t3][:, 256:512].rearrange("p (g i) -> p g i", g=8)
            ng, w0 = 8, 0
        else:
            src = bank[:, 0:256].rearrange("p (g t i) -> p g t i", g=4, t=2)
            cC = tb[t3][:, 128:256].rearrange("p (g i) -> p g i", g=4)
            cS = tb[t3][:, 384:512].rearrange("p (g i) -> p g i", g=4)
            ng, w0 = 4, 256
        wd = ng * 64
        u = rotu[0][:, 0:wd].rearrange("p (g t i) -> p g t i", g=ng, t=2)
        w = rotw[0][:, 0:wd].rearrange("p (g t i) -> p g t i", g=ng, t=2)
        o = qk[q3][:, w0:w0 + wd].rearrange("p (g t i) -> p g t i", g=ng, t=2)
        S.add("dve", lambda e: e.tensor_tensor(out=u, in0=src, in1=cC.unsqueeze(2).to_broadcast([128, ng, 2, 32]),
                                               op=ALU.mult),
              reads=[bb, b_tb[t3]], writes=[b_rotu[0]])
        S.add("dve", lambda e: e.tensor_tensor(out=w, in0=src, in1=cS.unsqueeze(2).to_broadcast([128, ng, 2, 32]),
                                               op=ALU.mult),
              reads=[bb, b_tb[t3]], writes=[b_rotw[0]])
        S.add(tt(o[:, :, 0, :], u[:, :, 0, :], w[:, :, 1, :], ALU.subtract),
              reads=[b_rotu[0], b_rotw[0]], writes=[b_qk[q3]])
        S.add(tt(o[:, :, 1, :], w[:, :, 0, :], u[:, :, 1, :], ALU.add),
              reads=[b_rotu[0], b_rotw[0]], writes=[b_qk[q3]])

    def ev_rv(c, bank, bb, c0=0):
        S.add(cp(rv[c % 3][:], bank[:, c0:c0 + 512]), reads=[bb], writes=[b_rv[c % 3]])

    def norm_q(c, g):
        if g != 0:
            return
        small = banks[7]
        S.add("pe", lambda e: e.matmul(small[:, 0:512], lhsT=ones_bf[:], rhs=sqq[0][:], start=True, stop=True),
              reads=[b_sqq[0], b_misc], writes=[b_smallA])
        S.add("dve", lambda e: e.tensor_reduce(out=Mq[c % 2][:, 0:1], in_=small[:, 0:512], axis=AX.X, op=ALU.max),
              reads=[b_smallA], writes=[b_Mq[c % 2]])

    def norm_k(c):
        small = banks[7]
        S.add("pe", lambda e: e.matmul(small[:, 0:128], lhsT=ones_bf[:], rhs=sqk[0][:], start=True, stop=True),
              reads=[b_sqk[0], b_misc], writes=[b_smallA])
        S.add("dve", lambda e: e.tensor_reduce(out=Mk[c % 3][:, 0:1], in_=small[:, 0:128], axis=AX.X, op=ALU.max),
              reads=[b_smallA], writes=[b_Mk[c % 3]])

    def ev_sqk(c, bank, bb):
        S.add("act", lambda e: e.activation(out=sqk[0][:], in_=bank[:, 0:128], func=AF.Square),
              reads=[bb], writes=[b_sqk[0]])

    def shift_chain(c):
        s2 = c % 2
        T_ = tmpc[s2]
        S.add("dve", lambda e: e.tensor_tensor(out=T_[:, 0:1], in0=Mk[c % 3][:, 0:1], in1=Mk[(c - 1) % 3][:, 0:1],
                                               op=ALU.max),
              reads=[b_Mk[c % 3], b_Mk[(c - 1) % 3]], writes=[b_tmpc[s2]])
        S.add("dve", lambda e: e.tensor_tensor(out=T_[:, 2:3], in0=T_[:, 0:1], in1=Mq[s2][:, 0:1], op=ALU.add),
              reads=[b_tmpc[s2], b_Mq[s2]], writes=[b_tmpc[s2]])
        S.add("dve", lambda e: e.tensor_scalar(out=negC[s2][:, 0:1], in0=T_[:, 2:3], scalar1=cst[:, 2:3],
                                               scalar2=negsinkmax[:], op0=ALU.mult, op1=ALU.min),
              reads=[b_tmpc[s2], b_misc], writes=[b_negC[s2]])
        S.add("act", lambda e: e.activation(out=esk[s2][:], in_=sinks2[:], func=AF.Exp, bias=negC[s2][:, 0:1],
                                            scale=1.0),
              reads=[b_const, b_misc, b_negC[s2]], writes=[b_esk[s2]])

    def st_P_prefix(c):
        bank, bb = alloc("A")
        mm_tok(bank, bb, 0, Wa, b_Wa, A_RK, 256, c)
        ev_rot(c, bank, bb, 1)
        bank, bb = alloc("A")
        mm_tok(bank, bb, 0, Wa, b_Wa, A_RV, 512, c)
        ev_rv(c, bank, bb)
        if c == NP - 1:
            bank, bb = alloc("A")
            mm_feat(bank, bb, 0, Wa, b_Wa, A_AK, c)
            mm_tok(bank, bb, 128, Wa, b_Wa, A_AV, 128, c)
            ev_kv(c, bank, bb)
            ev_sqk(c, bank, bb)
            norm_k(c)

    def st_P_main(c, part):
        s2 = c % 2
        if part == 1:
            bank, bb = alloc("A")
            for j in range(4):
                mm_feat(bank, bb, j * 128, Wb, b_Wb, B_AQ + j * 128, c)
            S.add(cp(aqT[s2][:], bank[:, 0:512]), reads=[bb], writes=[b_aqT[s2]])
            S.add("act", lambda e, bank=bank: e.activation(out=sqq[0][:], in_=bank[:, 0:512], func=AF.Square),
                  reads=[bb], writes=[b_sqq[0]])
            bank, bb = alloc("A")
            mm_feat(bank, bb, 0, Wa, b_Wa, A_AK, c)
            mm_tok(bank, bb, 128, Wa, b_Wa, A_AV, 128, c)
            ev_kv(c, bank, bb)
            ev_sqk(c, bank, bb)
            bank, bb = alloc("A")
            mm_tok(bank, bb, 0, Wa, b_Wa, A_RQ, 512, c)
            ev_rot(c, bank, bb, 2)
            bank, bb = alloc("A")
            mm_tok(bank, bb, 0, Wa, b_Wa, A_RV, 512, c)
            ev_rv(c, bank, bb)
            norm_q(c, 0)
            return
        for gi, w0 in ((0, B_AZ), (1, B_RZ)):
            bank, bb = alloc("A")
            mm_tok(bank, bb, 0, Wb, b_Wb, w0, 512, c)
            S.add("act", lambda e, bank=bank, gi=gi: e.activation(out=gate[s2][:, gi * 512:(gi + 1) * 512],
                                                                  in_=bank[:, 0:512], func=AF.Tanh, scale=0.5),
                  reads=[bb], writes=[b_gate[s2]])
            S.add("dve", lambda e, bank=bank, gi=gi: e.scalar_tensor_tensor(
                out=gate[s2][:, gi * 512:(gi + 1) * 512], in0=gate[s2][:, gi * 512:(gi + 1) * 512], scalar=1.0,
                in1=bank[:, 0:512], op0=ALU.add, op1=ALU.mult),
                reads=[bb, b_gate[s2]], writes=[b_gate[s2]])
            if gi == 0:
                norm_q(c, 1)
            else:
                norm_k(c)
                shift_chain(c)

    def st_R(c):
        s2 = c % 2
        bank, bb = alloc("A")
        pb = bank[:].bitcast(BF16)
        for j in range(4):
            S.add("pe", lambda e, j=j: e.transpose(pb[:, j * 128:(j + 1) * 128], qk[c % 3][:, j * 128:(j + 1) * 128],
                                                   ident_bf[:]),
                  reads=[b_qk[c % 3], b_const], writes=[bb])
        S.add(cp(qT[s2][:].rearrange("p a b -> p (a b)"), pb[:, 0:256]), reads=[bb], writes=[b_qT[s2]])
        for r in range(2):
            S.add(cp(kTp[s2][r][r * 64:(r + 1) * 64, :, :].rearrange("p a b -> p (a b)"),
                     pb[r * 64:(r + 1) * 64, 256:512]),
                  reads=[bb], writes=[b_kTp[s2]])

    def st_SC(c):
        s2 = c % 2
        m = c - NP
        sk_cur, sk_prev = c % NKV, (c - 1) % NKV
        for g in range(2):
            for blk in range(2):
                u = g * 2 + blk
                bank, bb = alloc("B")
                if blk == 0:
                    mk = masks[:, 0:512] if m == 0 else masks[:, 512:1024]
                    sk = sk_prev
                else:
                    mk = masks[:, 1024:1536]
                    sk = sk_cur
                use_mm_mask = (m == 0 and blk == 0) or not opts["mask_pool"]
                if use_mm_mask:
                    S.add("pe", lambda e, bank=bank, mk=mk: e.matmul(bank[:, 0:512], lhsT=ident_bf[:], rhs=mk,
                                                                     start=True, stop=False),
                          reads=[b_const, b_constB], writes=[bb])
                S.add("pe", lambda e, bank=bank, sk=sk, g=g, um=use_mm_mask: e.matmul(
                    bank[:, 0:512], lhsT=kTa[sk][g][:], rhs=aqT[s2][:], start=(not um), stop=True),
                    reads=[b_kTa[sk], b_aqT[s2]], writes=[bb])
                S.add("act", lambda e, u=u, bank=bank, g=g: e.activation(out=PT[s2][u][:], in_=bank[:, 0:512], func=AF.Exp,
                                                                         bias=negC[s2][:, 0:1], scale=ATT_SCALE),
                      reads=[bb, b_negC[s2]], writes=[b_PT[s2][u]])
                if not use_mm_mask:
                    if blk == 0:
                        fn = lambda e, u=u: e.affine_select(out=PT[s2][u][:], in_=PT[s2][u][:], pattern=[[0, 4], [-1, 128]],
                                                            compare_op=ALU.is_gt, fill=0.0, base=0, channel_multiplier=1)
                    else:
                        fn = lambda e, u=u: e.affine_select(out=PT[s2][u][:], in_=PT[s2][u][:], pattern=[[0, 4], [1, 128]],
                                                            compare_op=ALU.is_ge, fill=0.0, base=0, channel_multiplier=-1)
                    S.add("pool", fn, reads=[b_PT[s2][u]], writes=[b_PT[s2][u]], cost={"pool": 0.6})

    def st_scT(c):
        s2 = c % 2
        bank, bb = alloc("B")
        for h in range(4):
            j, r = h // 2, h % 2
            S.add("pe", lambda e, h=h, j=j, r=r: e.matmul(bank[:, h * 128:(h + 1) * 128], lhsT=kTp[s2][r][:, j, :],
                                                          rhs=qT[s2][:, j, :], start=True, stop=True),
                  reads=[b_kTp[s2], b_qT[s2]], writes=[bb])
        S.add("dve", lambda e: e.tensor_tensor(out=scD[s2][:], in0=bank[:, 0:512], in1=causal[:], op=ALU.mult),
              reads=[bb, b_constB], writes=[b_scD[s2]])

    def st_PV(c):
        s2 = c % 2
        sk_cur, sk_prev = c % NKV, (c - 1) % NKV
        pvb = []
        for g in range(2):
            bank, bb = alloc("B")
            for j in range(4):
                for blk in range(2):
                    u = g * 2 + blk
                    sk = sk_prev if blk == 0 else sk_cur
                    S.add("pe", lambda e, bank=bank, j=j, blk=blk, u=u, sk=sk, g=g: e.matmul(
                        bank[:, j * 65:(j + 1) * 65], lhsT=PT[s2][u][:, j * 128:(j + 1) * 128], rhs=Vaug[sk][:, g, :],
                        start=(blk == 0), stop=(blk == 1)),
                        reads=[b_PT[s2][u], b_Vaug[sk]], writes=[bb])
            pvb.append((bank, bb))
        for g, (bank, bb) in enumerate(pvb):
            v = bank[:, 0:260].rearrange("p (j d) -> p j d", j=4)
            S.add("dve", lambda e, g=g, v=v: e.tensor_tensor(out=den[s2][:, g * 4:(g + 1) * 4], in0=v[:, :, 64],
                                                            in1=esk[s2][:, g * 4:(g + 1) * 4], op=ALU.add),
                  reads=[bb, b_esk[s2]], writes=[b_den[s2]])
        S.add("dve", lambda e: e.reciprocal(out=den[s2][:, 8:16], in_=den[s2][:, 0:8]),
              reads=[b_den[s2]], writes=[b_den[s2]])
        for g, (bank, bb) in enumerate(pvb):
            v = bank[:, 0:260].rearrange("p (j d) -> p j d", j=4)
            S.add("dve", lambda e, g=g, v=v: e.tensor_tensor(
                out=tmpa[0][:, g * 256:(g + 1) * 256].rearrange("p (j d) -> p j d", j=4), in0=v[:, :, 0:64],
                in1=den[s2][:, 8 + g * 4:8 + (g + 1) * 4].unsqueeze(2).to_broadcast([128, 4, 64]), op=ALU.mult),
                reads=[bb, b_den[s2]], writes=[b_tmpa[0]])
        S.add(tt(mix[s2][:, 0:512], tmpa[0][:], gate[s2][:, 0:512], ALU.mult),
              reads=[b_tmpa[0], b_gate[s2]], writes=[b_mix[s2]])

    def st_KV(c):
        s2 = c % 2
        sn = (c + 1) % 2
        bank, bb = alloc("B")
        for j in range(2):
            S.add("pe", lambda e, j=j: e.matmul(bank[:, j * 256:(j + 1) * 256],
                                                lhsT=qk[c % 3][:, 256 + j * 128:256 + (j + 1) * 128],
                                                rhs=rv[c % 3][:, j * 256:(j + 1) * 256], start=True, stop=True),
                  reads=[b_qk[c % 3], b_rv[c % 3]], writes=[bb])
        kv = bank[:, 0:512].rearrange("p (j x) -> p j x", j=2)
        for r in range(2):
            S.add("dve", lambda e, r=r: e.tensor_tensor(out=Sst[r * 64:(r + 1) * 64, :, :],
                                                        in0=Sst[r * 64:(r + 1) * 64, :, :],
                                                        in1=kv[r * 64:(r + 1) * 64, :, r * 128:(r + 1) * 128],
                                                        op=ALU.add),
                  reads=[bb, b_S], writes=[b_S])
        S.add(tt(Sst[:].rearrange("p a b -> p (a b)"), Sst[:].rearrange("p a b -> p (a b)"), cdtab[:], ALU.mult),
              reads=[b_S, b_const], writes=[b_S])
        if is_main(c) or c == NP - 1:
            for r in range(2):
                S.add(cp(Sbfp[sn][r][r * 64:(r + 1) * 64, :, :], Sst[r * 64:(r + 1) * 64, :, :], ("act", "dve", "pool")),
                      reads=[b_S], writes=[b_Sbfp[sn]])

    def st_OR(c):
        s2 = c % 2
        bank, bb = alloc("B")
        for h in range(4):
            j, r = h // 2, h % 2
            S.add("pe", lambda e, h=h: e.matmul(bank[:, h * 128:(h + 1) * 128], lhsT=scD[s2][:, h * 128:(h + 1) * 128],
                                                rhs=rv[c % 3][:, h * 128:(h + 1) * 128], start=True, stop=False),
                  reads=[b_scD[s2], b_rv[c % 3]], writes=[bb])
            S.add("pe", lambda e, h=h, j=j, r=r: e.matmul(bank[:, h * 128:(h + 1) * 128], lhsT=qT[s2][:, j, :],
                                                          rhs=Sbfp[s2][r][:, j, :], start=False, stop=True),
                  reads=[b_qT[s2], b_Sbfp[s2]], writes=[bb])
        G = gns[s2]
        S.add(cp(orsb[0][:], bank[:, 0:512]), reads=[bb], writes=[b_orsb[0]])
        ov = orsb[0][:].rearrange("p (h e) -> p h e", h=4)
        for h in range(4):
            S.add("dve", lambda e, h=h: e.bn_stats(out=G[:, h * 6:(h + 1) * 6], in_=orsb[0][:, h * 128:(h + 1) * 128]),
                  reads=[b_orsb[0]], writes=[b_gns[s2]], cost={"dve": 0.2})
        for h in range(4):
            S.add("dve", lambda e, h=h: e.bn_aggr(out=G[:, 24 + 2 * h:26 + 2 * h], in_=G[:, h * 6:(h + 1) * 6]),
                  reads=[b_gns[s2]], writes=[b_gns[s2]], cost={"dve": 0.15})
        S.add("dve", lambda e: e.tensor_scalar(out=G[:, 32:36],
                                               in0=G[:, 24:32].rearrange("p (h t) -> p h t", t=2)[:, :, 1],
                                               scalar1=GN_EPS, scalar2=None, op0=ALU.add),
              reads=[b_gns[s2]], writes=[b_gns[s2]])
        S.add("pool", lambda e: e.tensor_tensor(out=G[:, 36:40], in0=G[:, 32:36], in1=neghalf[:, 0:4], op=ALU.pow),
              reads=[b_gns[s2], b_misc], writes=[b_gns[s2]])
        S.add(tt(gg[0][:], gate[s2][:, 512:1024], gngh[:], ALU.mult),
              reads=[b_gate[s2], b_gngh], writes=[b_gg[0]])
        S.add(tt(gg[0][:].rearrange("p (h e) -> p h e", h=4), gg[0][:].rearrange("p (h e) -> p h e", h=4),
                 G[:, 36:40].unsqueeze(2).to_broadcast([128, 4, 128]), ALU.mult),
              reads=[b_gg[0], b_gns[s2]], writes=[b_gg[0]])
        for h in range(4):
            S.add("dve", lambda e, h=h: e.scalar_tensor_tensor(
                out=mix[s2][:, 512 + h * 128:512 + (h + 1) * 128], in0=orsb[0][:, h * 128:(h + 1) * 128],
                scalar=G[:, 24 + 2 * h:25 + 2 * h], in1=gg[0][:, h * 128:(h + 1) * 128], op0=ALU.subtract, op1=ALU.mult),
                reads=[b_orsb[0], b_gns[s2], b_gg[0]], writes=[b_mix[s2]])

    def st_M(c):
        s2 = c % 2
        bank, bb = alloc("A")
        pb = bank[:].bitcast(BF16)
        for kb in range(8):
            S.add("pe", lambda e, kb=kb: e.transpose(pb[:, kb * 128:(kb + 1) * 128],
                                                     mix[s2][:, kb * 128:(kb + 1) * 128], ident_bf[:]),
                  reads=[b_mix[s2], b_const], writes=[bb])
        S.add(cp(mixT[s2][:], pb[:, 0:1024]), reads=[bb], writes=[b_mixT[s2]])

    def st_O(c):
        s2, sx = c % 2, c % NX
        m = c - NP
        for half in range(2):
            bank, bb = alloc("A")
            for kb in range(8):
                S.add("pe", lambda e, kb=kb, bank=bank, half=half: e.matmul(
                    bank[:, 0:512], lhsT=mixT[s2][:, kb * 128:(kb + 1) * 128],
                    rhs=Wo[:, kb, half * 512:(half + 1) * 512], start=(kb == 0), stop=(kb == 7)),
                    reads=[b_mixT[s2]] + b_Wo, writes=[bb])
            S.add("dve", lambda e, bank=bank, half=half: e.tensor_tensor(
                out=yy[s2][:, half * 512:(half + 1) * 512], in0=bank[:, 0:512],
                in1=xs[sx][:, half * 512:(half + 1) * 512], op=ALU.add),
                reads=[bb, b_xs[sx]], writes=[b_yy[s2]])
        st = stat2[s2]
        S.add("act", lambda e: e.activation(out=junk[:], in_=yy[s2][:], func=AF.Square, accum_out=st[:, 0:1]),
              reads=[b_yy[s2]], writes=[b_junk, b_stat2[s2]])
        S.add("dve", lambda e: e.tensor_scalar(out=st[:, 1:2], in0=st[:, 0:1], scalar1=1.0 / D, scalar2=RMS_EPS,
                                               op0=ALU.mult, op1=ALU.add),
              reads=[b_stat2[s2]], writes=[b_stat2[s2]])
        S.add("pool", lambda e: e.tensor_tensor(out=st[:, 2:3], in0=st[:, 1:2], in1=neghalf[:, 0:1], op=ALU.pow),
              reads=[b_stat2[s2], b_misc], writes=[b_stat2[s2]])
        S.add("dve", lambda e: e.scalar_tensor_tensor(out=yy[s2][:], in0=yy[s2][:], scalar=st[:, 2:3], in1=fg_rep[:],
                                                      op0=ALU.mult, op1=ALU.mult),
              reads=[b_yy[s2], b_stat2[s2], b_constB], writes=[b_yy[s2]])
        S.add("sp", dma(out_d[m * CH:(m + 1) * CH, :], yy[s2][:]), reads=[b_yy[s2]], dmaq=f"st{s2}")

    def run_stage(name, c):
        if c < 0 or c >= NT:
            return
        main = is_main(c)
        S.cur_tag = name + ('' if main else '_pre')
        S.cur_chunk = c
        if name == "load":
            st_load(c)
        elif name == "front":
            st_front(c)
        elif name == "T":
            st_T(c)
        elif name == "P":
            if main:
                st_P_main(c, 1)
            else:
                st_P_prefix(c)
        elif name == "P2":
            if main:
                st_P_main(c, 2)
        elif name == "KV":
            st_KV(c)
        elif not main:
            return
        elif name == "R":
            st_R(c)
        elif name == "SC":
            st_SC(c)
        elif name == "scT":
            st_scT(c)
        elif name == "PV":
            st_PV(c)
        elif name == "OR":
            st_OR(c)
        elif name == "M":
            st_M(c)
        elif name == "O":
            st_O(c)

    order = opts["order"]
    t_first = -4 - opts["pre_lead"]
    for t in range(t_first, NT + 3):
        if t == t_first + 2:
            for task in early_w:
                load_weight(*task)
            emit_late_consts()
            emit_gngh()
        if t >= 0 and pending_w:
            for _ in range(max(1, -(-24 // max(1, NP - 1)))):
                if pending_w:
                    load_weight(*pending_w.pop(0))
        for name, off in order:
            if name in ("load", "front"):
                cp_ = t - off + opts["pre_lead"]
                if 0 <= cp_ < NP:
                    run_stage(name, cp_)
                cm_ = t - off
                if cm_ >= NP:
                    run_stage(name, cm_)
            else:
                run_stage(name, t - off)

    if SCHEDULE:
        S.schedule(window=opts["window"], urgency=opts["urgency"], bin_us=opts["bin_us"], bias=opts["bias"], bias_pre=opts["bias_pre"])
        build_program.last_makespan = S.makespan
    S.finalize()
    S.emit(nc, block, sems, [f"st{i}" for i in range(2)])
    es.close()
    return nc


def _perm_ret_qk():
    idx = []
    for h in range(4):
        base = h * 64
        idx += [base + 2 * i for i in range(32)] + [base + 2 * i + 1 for i in range(32)]
    return np.array(idx)


def _const_tables(pos0_list, first_half):
    theta = (1.0 / (10000.0 ** np.linspace(0.0, 1.0, 32, dtype=np.float32))).astype(np.float32)
    gam = 1.0 - 2.0 ** (-5.0 - np.arange(4, dtype=np.float64))
    i = np.arange(128, dtype=np.float64)
    dq = gam[None, :] ** (i[:, None] + 1.0)
    dk = gam[None, :] ** (-(i[:, None] + 1.0)) / 8.0
    tabs = np.zeros((len(pos0_list), 128, 4, 4, 32), np.float32)
    for n, p0 in enumerate(pos0_list):
        pos = (p0 + np.arange(128)).astype(np.float32)
        ang = pos[:, None] * theta[None, :]
        c, s = np.cos(ang).astype(np.float64), np.sin(ang).astype(np.float64)
        tabs[n, :, 0] = c[:, None, :] * dq[:, :, None]
        tabs[n, :, 1] = c[:, None, :] * dk[:, :, None]
        tabs[n, :, 2] = s[:, None, :] * dq[:, :, None]
        tabs[n, :, 3] = s[:, None, :] * dk[:, :, None]
    return tabs.reshape(len(pos0_list) * 128, 512)


def _static_consts():
    j = np.arange(128)[:, None]
    i = np.arange(128)[None, :]
    mp = np.where(j > i, 0.0, NEG).astype(np.float32)
    mc = np.where(j <= i, 0.0, NEG).astype(np.float32)
    mp0 = np.full((128, 128), NEG, np.float32)
    t4 = lambda a: np.tile(a, (1, 4))
    causal = t4(np.where(j <= i, 1.0, 0.0).astype(np.float32))
    gam = 1.0 - 2.0 ** (-5.0 - np.arange(4, dtype=np.float64))
    cd = gam ** 128.0
    cdtab = np.zeros((128, 2, 128), np.float32)
    for r in range(2):
        for jj in range(2):
            cdtab[r * 64:(r + 1) * 64, jj, :] = cd[2 * jj + r]
    gsel = np.zeros((128, 2, 128), np.float32)
    gsel[0:64, 0, :] = 1.0
    gsel[64:128, 1, :] = 1.0
    return dict(mp0=t4(mp0), mp=t4(mp), mc=t4(mc), causal=causal, cdtab=cdtab.reshape(128, 256),
                ident=np.eye(128, dtype=np.float32), gsel=gsel.reshape(128, 256))


def _prep_weights(w_in, w_out):
    w = np.asarray(w_in, np.float32)[0]
    cuts = np.cumsum([512, 128, 128, 512, 256, 256, 512, 512])[:-1]
    aq, ak, av, az, rq, rk, rv, rz = np.split(w, cuts, axis=1)
    p = _perm_ret_qk()
    rq, rk = rq[:, p], rk[:, p]
    aqp = np.concatenate([np.concatenate([aq[:, j * 64:(j + 1) * 64], aq[:, (j + 4) * 64:(j + 5) * 64]], 1)
                          for j in range(4)], 1)
    wa = np.ascontiguousarray(np.concatenate([ak, av, rq, rk, rv], 1))
    wb = np.ascontiguousarray(np.concatenate([aqp, az, rz], 1))
    wo = np.ascontiguousarray(np.asarray(w_out, np.float32)[0])
    return wa, wb, wo


def _core_inputs(xb, first_half, NP, NM, pos_main0, shared):
    main = xb[pos_main0:pos_main0 + NM * CH]
    if first_half:
        pre = np.zeros((NP * CH, D), np.float32)
        pos_pre0 = 0
    else:
        pre = xb[pos_main0 - NP * CH:pos_main0]
        pos_pre0 = pos_main0 - NP * CH
    pos0 = [pos_pre0 + n * CH for n in range(NP)] + [pos_main0 + n * CH for n in range(NM)]
    sc = shared["sc"]
    masks = np.concatenate([sc["mp0"] if first_half else sc["mp"], sc["mp"], sc["mc"]], 1)
    d = dict(shared["common"])
    d["x_all"] = np.ascontiguousarray(np.concatenate([pre, main], 0), dtype=np.float32)
    d["tabs"] = _const_tables(pos0, first_half)
    d["masks"] = masks.astype(ml_dtypes.bfloat16)
    return d


def _shared(norm_g, w_in, att_sinks, ret_gn_g, w_out, final_g):
    sc = _static_consts()
    wa, wb, wo = _prep_weights(w_in, w_out)
    rep = lambda v: np.ascontiguousarray(np.broadcast_to(np.asarray(v, np.float32).reshape(1, -1), (128, np.asarray(v).size)))
    common = dict(w_a=wa, w_b=wb, w_o=wo, g_col=np.ascontiguousarray(np.asarray(norm_g, np.float32).reshape(8, 128).T), fg_rep=rep(final_g), gng_rep=rep(ret_gn_g),
                  sinks_rep=rep(att_sinks), causal01=sc["causal"], cdtab=sc["cdtab"],
                  ident_bf=sc["ident"].astype(ml_dtypes.bfloat16), gsel=sc["gsel"].astype(ml_dtypes.bfloat16))
    return dict(sc=sc, common=common)


_PROG = {}


def kernel(x, norm_g, w_in, att_sinks, ret_gn_g, w_out, final_g):
    x = np.asarray(x, np.float32)
    B = x.shape[0]
    NP = NM = HALF // CH
    shared = _shared(norm_g, w_in, att_sinks, ret_gn_g, w_out, final_g)
    in_maps = []
    for c in range(NCORES):
        b, half = c // 2, c % 2
        in_maps.append(_core_inputs(x[b], half == 0, NP, NM, half * HALF, shared))
    if "nc" not in _PROG:
        _PROG["nc"] = build_program(NP, NM)
    res = run_bass_kernel_spmd(_PROG["nc"], in_maps, core_ids=list(range(NCORES)))
    out = np.empty((B, SEQ, D), np.float32)
    for c in range(NCORES):
        b, half = c // 2, c % 2
        out[b, half * HALF:(half + 1) * HALF] = res.results[c]["out"]
    return out
```
